# Optimizing a Trainium2 kernel written in Bass

```python
import jax, jax.numpy as jnp
from jax import lax
import numpy as np

D_MODEL = 2048
BATCH = 4
SEQ = 2048
DEPTH = 2
DEC_BATCH = 128
DEC_SEQ = 4
PAST_LEN = 16384
PAGE_SIZE = 128

HGRN_WIDTH = D_MODEL // 2
HGRN_HEAD_DIM = 128
HGRN_HEADS = HGRN_WIDTH // HGRN_HEAD_DIM
HGRN_CHUNK = 64
POOL_WIDTH = D_MODEL - HGRN_WIDTH
POOL_WINDOWS = (2, 4, 8, 16)
POOL_GROUPS = len(POOL_WINDOWS)
POOL_GROUP_DIM = POOL_WIDTH // POOL_GROUPS
POOL_BUF = max(POOL_WINDOWS) - 1
IN_PROJ_WIDTH = 4 * HGRN_WIDTH + POOL_WIDTH
PEER_HEADS = 8
PEER_N_KEYS = 128
PEER_N_EXPERTS = PEER_N_KEYS ** 2
PEER_TOPK = 16
PEER_QUERY_DIM = 256
PEER_HALF = PEER_QUERY_DIM // 2
PEER_BLOCK = 128
N_MOD = 6
EPS = 1e-6

kernel_name = "hgrn2_pool_peer_hybrid_step"


def rmsnorm(x, g):
    xf = x.astype(jnp.float32)
    y = xf * lax.rsqrt(jnp.mean(xf * xf, axis=-1, keepdims=True) + EPS)
    return (y * g.astype(jnp.float32)).astype(x.dtype)


def modulate(x, shift, scale):
    return x * (1 + scale[:, None]) + shift[:, None]


def hgrn2_chunked(q, k, v, log_f, s0, chunk):
    dt = v.dtype
    B, T, H, DK = q.shape
    DV = v.shape[-1]
    n = -(-T // chunk)
    pad = n * chunk - T

    def prep(a):
        a = jnp.pad(a.astype(jnp.float32), ((0, 0), (0, pad), (0, 0), (0, 0)))
        return a.reshape(B, n, chunk, H, a.shape[-1]).swapaxes(0, 1)

    mask = jnp.tril(jnp.ones((chunk, chunk), dtype=bool))[None, :, :, None, None]

    def step(S, inp):
        qc, kc, vc, gc = inp
        G = jnp.cumsum(gc, axis=1)
        o_inter = jnp.einsum('bthk,bhkv->bthv', qc * jnp.exp(G), S)
        diff = G[:, :, None] - G[:, None, :]
        decay = jnp.exp(jnp.where(mask, diff, -jnp.inf))
        A = jnp.einsum('bthk,bshk,btshk->bths', qc, kc, decay)
        o_intra = jnp.einsum('bths,bshv->bthv', A, vc)
        g_end = G[:, -1]
        k_dec = kc * jnp.exp(g_end[:, None] - G)
        S_new = jnp.exp(g_end)[..., None] * S + jnp.einsum('bshk,bshv->bhkv', k_dec, vc)
        return S_new, o_inter + o_intra

    S_fin, o = lax.scan(step, s0.astype(jnp.float32), (prep(q), prep(k), prep(v), prep(log_f)))
    o = o.swapaxes(0, 1).reshape(B, n * chunk, H, DV)[:, :T]
    return o.astype(dt), S_fin.astype(s0.dtype)


def multiscale_pool(v, buf, start_pos):
    B, T, W = v.shape
    full = jnp.concatenate([buf.astype(v.dtype), v], axis=1)
    cs = jnp.pad(jnp.cumsum(full.astype(jnp.float32), axis=1), ((0, 0), (1, 0), (0, 0)))
    pos = start_pos + jnp.arange(T, dtype=jnp.int32)
    vf = v.astype(jnp.float32)
    outs = []
    for gi, w in enumerate(POOL_WINDOWS):
        sl = slice(gi * POOL_GROUP_DIM, (gi + 1) * POOL_GROUP_DIM)
        hi = cs[:, POOL_BUF + 1:POOL_BUF + 1 + T, sl]
        lo = cs[:, POOL_BUF + 1 - w:POOL_BUF + 1 - w + T, sl]
        cnt = jnp.minimum(w, pos + 1).astype(jnp.float32)
        outs.append((hi - lo) / cnt[None, :, None] - vf[..., sl])
    pooled = jnp.stack(outs, axis=2)
    return pooled, full[:, -POOL_BUF:]


def peer_ffn(h, wq, keys, u_tab, v_tab):
    B, T, D = h.shape
    N = B * T
    xt = h.reshape(N, D)
    qry = (xt @ wq).reshape(N, PEER_HEADS, 2, PEER_HALF).astype(jnp.float32)
    qry = qry * lax.rsqrt(jnp.mean(qry * qry, axis=-1, keepdims=True) + EPS)
    scores = jnp.einsum('nhpd,hpkd->nhpk', qry, keys.astype(jnp.float32))
    s1, i1 = lax.top_k(scores[:, :, 0], PEER_TOPK)
    s2, i2 = lax.top_k(scores[:, :, 1], PEER_TOPK)
    cand = (s1[..., :, None] + s2[..., None, :]).reshape(N, PEER_HEADS, PEER_TOPK * PEER_TOPK)
    cidx = (i1[..., :, None] * PEER_N_KEYS + i2[..., None, :]).reshape(N, PEER_HEADS, PEER_TOPK * PEER_TOPK)
    top_s, top_pos = lax.top_k(cand, PEER_TOPK)
    eidx = jnp.take_along_axis(cidx, top_pos, axis=-1)
    gate = jax.nn.softmax(top_s, axis=-1).astype(h.dtype)
    nb = -(-N // PEER_BLOCK)
    pad = nb * PEER_BLOCK - N
    xb = jnp.pad(xt, ((0, pad), (0, 0))).reshape(nb, PEER_BLOCK, D)
    eb = jnp.pad(eidx, ((0, pad), (0, 0), (0, 0))).reshape(nb, PEER_BLOCK, PEER_HEADS, PEER_TOPK)
    gb = jnp.pad(gate, ((0, pad), (0, 0), (0, 0))).reshape(nb, PEER_BLOCK, PEER_HEADS, PEER_TOPK)

    def block(args):
        xs, es, gs = args
        u = u_tab[es]
        a = jax.nn.gelu(jnp.einsum('bd,bhkd->bhk', xs, u), approximate=False) * gs
        return jnp.einsum('bhk,bhkd->bd', a, v_tab[es])

    out = lax.map(block, (xb, eb, gb)).reshape(nb * PEER_BLOCK, D)[:N]
    return out.reshape(B, T, D)


def hybrid_layer(x, c, hgrn_state, pool_buf, start_pos, lb, w_ada, b_ada, norm1_g, norm2_g,
                 w_in, w_out, hgrn_norm_g, pool_w, pool_b, pool_scale,
                 peer_wq, peer_keys, peer_u, peer_v):
    B, T, D = x.shape
    mod = jax.nn.silu(c) @ w_ada + b_ada
    sh1, sc1, g1, sh2, sc2, g2 = jnp.split(mod, N_MOD, axis=-1)
    h = modulate(rmsnorm(x, norm1_g), sh1, sc1)
    z = h @ w_in
    zq, zf, zi, zg, zp = jnp.split(
        z, [HGRN_WIDTH, 2 * HGRN_WIDTH, 3 * HGRN_WIDTH, 4 * HGRN_WIDTH], axis=-1)

    def heads(a):
        return a.reshape(B, T, HGRN_HEADS, HGRN_HEAD_DIM)

    q = jax.nn.silu(heads(zq))
    zf32 = heads(zf).astype(jnp.float32)
    lbh = lb.reshape(HGRN_HEADS, HGRN_HEAD_DIM).astype(jnp.float32)
    log_f = jnp.logaddexp(jnp.log(lbh), jnp.log1p(-lbh) + jax.nn.log_sigmoid(zf32))
    k = (1 - lbh) * jax.nn.sigmoid(-zf32)
    o, new_state = hgrn2_chunked(q, k, heads(zi), log_f, hgrn_state, min(HGRN_CHUNK, T))
    o_a = (rmsnorm(o, hgrn_norm_g) * jax.nn.silu(heads(zg))).reshape(B, T, HGRN_WIDTH)

    pooled, new_buf = multiscale_pool(zp, pool_buf, start_pos)
    pooled = pooled.astype(x.dtype)
    o_b = (jnp.einsum('btgi,gio->btgo', pooled, pool_w)
           + pool_b.reshape(POOL_GROUPS, POOL_GROUP_DIM)) * pool_scale.reshape(POOL_GROUPS, POOL_GROUP_DIM)
    o_b = o_b.reshape(B, T, POOL_WIDTH)

    mix = jnp.concatenate([o_a, o_b], axis=-1) @ w_out
    x = x + g1[:, None] * mix

    h2 = modulate(rmsnorm(x, norm2_g), sh2, sc2)
    x = x + g2[:, None] * peer_ffn(h2, peer_wq, peer_keys, peer_u, peer_v)
    return x, new_state, new_buf


def setup_inputs(seed: int = 0) -> dict:
    key = jax.random.key(seed)
    ks = jax.random.split(key, 24)

    def nrm(k, shape, scale):
        return jax.random.normal(k, shape, jnp.float32) * scale

    D = D_MODEL
    return {
        "x_prompt": nrm(ks[0], (BATCH, SEQ, D), 1.0),
        "x_sample": nrm(ks[1], (DEC_BATCH, DEC_SEQ, D), 1.0),
        "c_prompt": nrm(ks[2], (BATCH, D), 1.0),
        "c_sample": nrm(ks[3], (DEC_BATCH, D), 1.0),
        "state_hgrn": nrm(ks[4], (DEPTH, DEC_BATCH, HGRN_HEADS, HGRN_HEAD_DIM, HGRN_HEAD_DIM), 0.2),
        "state_pool": nrm(ks[5], (DEPTH, DEC_BATCH, POOL_BUF, POOL_WIDTH), 1.0),
        "w_ada": nrm(ks[6], (DEPTH, D, N_MOD * D), 0.5 * D ** -0.5),
        "b_ada": nrm(ks[7], (DEPTH, N_MOD * D), 0.01),
        "norm1_g": 1.0 + nrm(ks[8], (DEPTH, D), 0.02),
        "norm2_g": 1.0 + nrm(ks[9], (DEPTH, D), 0.02),
        "w_in": nrm(ks[10], (DEPTH, D, IN_PROJ_WIDTH), D ** -0.5),
        "w_out": nrm(ks[11], (DEPTH, D, D), D ** -0.5),
        "lb_logits": nrm(ks[12], (DEPTH, HGRN_WIDTH), 1.0),
        "hgrn_norm_g": 1.0 + nrm(ks[13], (DEPTH, HGRN_HEAD_DIM), 0.02),
        "pool_w": nrm(ks[14], (DEPTH, POOL_GROUPS, POOL_GROUP_DIM, POOL_GROUP_DIM), POOL_GROUP_DIM ** -0.5),
        "pool_b": nrm(ks[15], (DEPTH, POOL_WIDTH), 0.01),
        "pool_scale": 1.0 + nrm(ks[16], (DEPTH, POOL_WIDTH), 0.02),
        "peer_wq": nrm(ks[17], (DEPTH, D, PEER_HEADS * PEER_QUERY_DIM), D ** -0.5),
        "peer_keys": nrm(ks[18], (DEPTH, PEER_HEADS, 2, PEER_N_KEYS, PEER_HALF), PEER_HALF ** -0.5),
        "peer_u": nrm(ks[19], (DEPTH, PEER_N_EXPERTS, D), D ** -0.5),
        "peer_v": nrm(ks[20], (DEPTH, PEER_N_EXPERTS, D), PEER_HEADS ** -0.5),
        "final_g": 1.0 + nrm(ks[21], (D,), 0.02),
        "w_ada_final": nrm(ks[22], (D, 2 * D), 0.5 * D ** -0.5),
        "b_ada_final": nrm(ks[23], (2 * D,), 0.01),
    }


def reference(x_prompt, x_sample, c_prompt, c_sample, state_hgrn, state_pool,
              w_ada, b_ada, norm1_g, norm2_g, w_in, w_out, lb_logits, hgrn_norm_g,
              pool_w, pool_b, pool_scale, peer_wq, peer_keys, peer_u, peer_v,
              final_g, w_ada_final, b_ada_final):
    lb_all = jnp.cumsum(jax.nn.softmax(lb_logits.astype(jnp.float32), axis=0), axis=0)
    lb_all = lb_all - lb_all[0:1]

    zero_state = jnp.zeros((BATCH, HGRN_HEADS, HGRN_HEAD_DIM, HGRN_HEAD_DIM), x_prompt.dtype)
    zero_buf = jnp.zeros((BATCH, POOL_BUF, POOL_WIDTH), x_prompt.dtype)

    xp, xs = x_prompt, x_sample
    sp_list, bp_list, ss_list, bs_list = [], [], [], []
    for l in range(DEPTH):
        w = (lb_all[l], w_ada[l], b_ada[l], norm1_g[l], norm2_g[l], w_in[l], w_out[l],
             hgrn_norm_g[l], pool_w[l], pool_b[l], pool_scale[l],
             peer_wq[l], peer_keys[l], peer_u[l], peer_v[l])
        xp, sp, bp = hybrid_layer(xp, c_prompt, zero_state, zero_buf, 0, *w)
        xs, ss, bs = hybrid_layer(xs, c_sample, state_hgrn[l], state_pool[l], PAST_LEN, *w)
        sp_list.append(sp)
        bp_list.append(bp)
        ss_list.append(ss)
        bs_list.append(bs)

    def final_norm(x, c):
        mod = jax.nn.silu(c) @ w_ada_final + b_ada_final
        sh, sc = jnp.split(mod, 2, axis=-1)
        return modulate(rmsnorm(x, final_g), sh, sc)

    y_prompt = final_norm(xp, c_prompt)
    y_sample = final_norm(xs, c_sample)
    return (y_prompt, y_sample, jnp.stack(sp_list), jnp.stack(bp_list), jnp.stack(ss_list), jnp.stack(bs_list))
```

```python
import numpy as np
from contextlib import ExitStack
import concourse.bass as bass
import concourse.mybir as mybir
from concourse.bass_utils import run_bass_kernel_spmd

F32 = mybir.dt.float32
BF16 = mybir.dt.bfloat16
U32 = mybir.dt.uint32
AF = mybir.ActivationFunctionType
ALU = mybir.AluOpType
AX = mybir.AxisListType

D = 2048
KC = 16
NH = 8
NPH = 8
NK = 128
TOPK = 16
EPS = 1e-6
NEG = -1.0e30

ENGS = ("act", "pool", "dve", "pe")
N_DMA_SEM = 40
EPOCH = 30000
SAME_ENGINE_SYNC = True


class Dep:
    __slots__ = ("lw", "rd")

    def __init__(self):
        self.lw = None
        self.rd = {}


class V:
    __slots__ = ("ap", "deps")

    def __init__(self, ap, deps):
        self.ap = ap
        self.deps = deps

    def __getitem__(self, k):
        return V(self.ap[k], self.deps)

    def bc(self, shape):
        return V(self.ap.to_broadcast(list(shape)), self.deps)

    def re(self, pat, **kw):
        return V(self.ap.rearrange(pat, **kw), self.deps)

    def unsq(self, ax):
        return V(self.ap.unsqueeze(ax), self.deps)

    def bitcast(self, dt):
        return V(self.ap.bitcast(dt), self.deps)

    @property
    def shape(self):
        return self.ap.shape


class Buf:
    def __init__(self, t, tracked=True, is_dram=False):
        self.t = t
        self.tracked = tracked
        self.is_dram = is_dram
        self.dep = Dep()
        self.regions = {}

    def _base(self):
        return self.t.ap() if self.is_dram else self.t

    def __getitem__(self, k):
        return V(self._base()[k], [self.dep] if self.tracked else [])

    def v(self):
        return self[:]

    def r(self, *keys):
        deps = []
        for key in keys:
            if key not in self.regions:
                self.regions[key] = Dep()
            deps.append(self.regions[key])
        return V(self._base()[:], deps)


class Prog:
    def __init__(self, nc):
        self.nc = nc
        self.ops = {e: [] for e in ("sync", "act", "pool", "dve", "pe")}
        self.nsem = N_DMA_SEM
        self.esem = {}
        self.cnt = {}
        for e in ENGS:
            self.esem[e] = self.nsem
            self.nsem += 1
            self.cnt[e] = 0
        self.all_esems = {e: [self.esem[e]] for e in ENGS}
        self.dma_cnt = [0] * N_DMA_SEM
        self.dma_rr = 0
        self.waited = {e: {} for e in self.ops}
        self.pending = {e: {} for e in self.ops}

    def barrier(self):
        evs = []
        for i in range(N_DMA_SEM):
            if self.dma_cnt[i]:
                evs.append((i, 16 * self.dma_cnt[i]))
        for e in ENGS:
            if self.cnt[e]:
                evs.append((self.esem[e], self.cnt[e]))
        for e in self.ops:
            for s, v in evs:
                if self.pending[e].get(s, 0) < v:
                    self.pending[e][s] = v

    def sb(self, name, shape, dt):
        return Buf(self.nc.alloc_sbuf_tensor(name, list(shape), dt))

    def ps(self, name, shape, dt=F32):
        return Buf(self.nc.alloc_psum_tensor(name, list(shape), dt))

    def dram(self, name, shape, dt, kind):
        return Buf(self.nc.dram_tensor(name, list(shape), dt, kind=kind),
                   tracked=(kind == "Internal"), is_dram=True)

    def _record(self, eng, fn, outs, ins, is_dma):
        waits = dict(self.pending[eng])
        self.pending[eng] = {}

        def need(ev):
            if ev is None:
                return
            s, v = ev
            if waits.get(s, 0) < v:
                waits[s] = v

        for x in ins:
            for d in x.deps:
                need(d.lw)
        for x in outs:
            for d in x.deps:
                need(d.lw)
                for s, v in d.rd.items():
                    need((s, v))
        if is_dma:
            sid = self.dma_rr
            self.dma_rr = (self.dma_rr + 1) % N_DMA_SEM
            if self.dma_cnt[sid] > 0:
                need((sid, 16 * self.dma_cnt[sid]))
            self.dma_cnt[sid] += 1
            ev = (sid, 16 * self.dma_cnt[sid])
            inc = (sid, 16)
        else:
            if self.cnt[eng] >= EPOCH:
                self.esem[eng] = self.nsem
                self.all_esems[eng].append(self.nsem)
                self.nsem += 1
                self.cnt[eng] = 0
            sid = self.esem[eng]
            self.cnt[eng] += 1
            ev = (sid, self.cnt[eng])
            inc = (sid, 1)
        wl = []
        wd = self.waited[eng]
        for s, v in waits.items():
            if (not is_dma) and s in self.all_esems.get(eng, ()):
                if eng == "pe" or not SAME_ENGINE_SYNC:
                    continue
                if s != sid or (ev[1] - v) >= 3:
                    continue
            if wd.get(s, 0) >= v:
                continue
            wd[s] = v
            wl.append((s, v))
        self.ops[eng].append((wl, fn, inc))
        for x in ins:
            for d in x.deps:
                if d.rd.get(ev[0], 0) < ev[1]:
                    d.rd[ev[0]] = ev[1]
        for x in outs:
            for d in x.deps:
                d.lw = ev
                d.rd = {}
        return ev

    def dma(self, q, out, in_):
        return self._record(q, lambda e: e.dma_start(out=out.ap, in_=in_.ap), [out], [in_], True)

    def act(self, out, in_, func, scale=1.0, bias=0.0, extra=()):
        def fn(e):
            return e.activation(out=out.ap, in_=in_.ap, func=func, scale=scale, bias=bias)
        return self._record("act", fn, [out], [in_, *extra], False)

    def op(self, eng, fn, outs, ins):
        return self._record(eng, fn, outs, ins, False)

    def tt(self, out, a, b, op, eng="dve"):
        return self.op(eng, lambda e: e.tensor_tensor(out=out.ap, in0=a.ap, in1=b.ap, op=op), [out], [a, b])

    def ts(self, out, a, s1, s2, op0, op1=None, eng="dve"):
        ins = [a] + [s for s in (s1, s2) if isinstance(s, V)]
        a1 = s1.ap if isinstance(s1, V) else s1
        a2 = s2.ap if isinstance(s2, V) else s2

        def fn(e):
            if op1 is None:
                return e.tensor_scalar(out=out.ap, in0=a.ap, scalar1=a1, scalar2=None, op0=op0)
            return e.tensor_scalar(out=out.ap, in0=a.ap, scalar1=a1, scalar2=a2, op0=op0, op1=op1)
        return self.op(eng, fn, [out], ins)

    def stt(self, out, a, s, b, op0, op1):
        ins = [a, b] + ([s] if isinstance(s, V) else [])
        sa = s.ap if isinstance(s, V) else s
        return self.op("dve", lambda e: e.scalar_tensor_tensor(out=out.ap, in0=a.ap, scalar=sa, in1=b.ap,
                                                               op0=op0, op1=op1), [out], ins)

    def copy(self, out, in_, eng="dve"):
        if eng == "act":
            return self.act(out, in_, AF.Copy)
        return self.op(eng, lambda e: e.tensor_copy(out=out.ap, in_=in_.ap), [out], [in_])

    def memset(self, out, val, eng="dve"):
        return self.op(eng, lambda e: e.memset(out.ap, val), [out], [])

    def red(self, out, in_, op=ALU.add, axis=AX.X):
        return self.op("dve", lambda e: e.tensor_reduce(out=out.ap, in_=in_.ap, axis=axis, op=op), [out], [in_])

    def recip(self, out, in_):
        return self.op("dve", lambda e: e.reciprocal(out=out.ap, in_=in_.ap), [out], [in_])

    def mm(self, out, lhsT, rhs, start=True, stop=True):
        return self.op("pe", lambda e: e.matmul(out.ap, lhsT.ap, rhs.ap, start=start, stop=stop),
                       [out], [lhsT, rhs])

    def tr(self, out, in_, ident):
        return self.op("pe", lambda e: e.transpose(out.ap, in_.ap, ident.ap), [out], [in_, ident])

    def emit(self):
        nc = self.nc
        fin = []
        for i in range(N_DMA_SEM):
            if self.dma_cnt[i]:
                fin.append((i, 16 * self.dma_cnt[i]))
        for e in ENGS:
            if self.cnt[e]:
                fin.append((self.esem[e], self.cnt[e]))
        with ExitStack() as es:
            es.enter_context(nc.allow_non_contiguous_dma(reason="small strided parameter loads"))
            sems = [es.enter_context(nc.semaphore(f"s{i}")) for i in range(self.nsem)]
            block = es.enter_context(nc.Block())

            def run(eng_name, e):
                for wl, fn, inc in self.ops[eng_name]:
                    for s, v in wl:
                        e.wait_ge(sems[s], v)
                    ins = fn(e)
                    ins.then_inc(sems[inc[0]], inc[1])
                if eng_name == "sync":
                    for s, v in fin:
                        e.wait_ge(sems[s], v)
                    e.nop()

            @block.sync
            def _(e):
                run("sync", e)

            @block.scalar
            def _(e):
                run("act", e)

            @block.gpsimd
            def _(e):
                run("pool", e)

            @block.vector
            def _(e):
                run("dve", e)

            @block.tensor
            def _(e):
                run("pe", e)


def build(T, NS, L, dbg=False):
    nc = bass.Bass("TRN2", target_bir_lowering=False)
    P = Prog(nc)
    NSR = 4 * NS
    TT = T + NSR
    NE = NK * NK
    half = T // 2 if T >= 256 else T
    sgs = []
    if T >= 256:
        sgs.append([("p", t0, 128) for t0 in range(0, half, 128)])
        sgs.append([("p", t0, 128) for t0 in range(half, T, 128)] + [("s", 0, NSR)])
    else:
        sgs.append([("p", t0, 128) for t0 in range(0, T, 128)] + [("s", 0, NSR)])
    NSG = max(sum(r for _, _, r in sg) for sg in sgs)
    NTL = max(len(sg) for sg in sgs)

    I = lambda n, s, dt=F32: P.dram(n, s, dt, "ExternalInput")
    O = lambda n, s: P.dram(n, s, F32, "ExternalOutput")
    xp = I("xp", [T, D]); xs = I("xs", [NSR, D]); c17 = I("c17", [NS + 1, D])
    sh_in = I("sh_in", [L, NS, NH, 128, 128]); sp_in = I("sp_in", [L, NS, 15, 1024])
    w_ada = I("w_ada", [L, D, 6 * D]); b_ada = I("b_ada", [L, 6 * D])
    n1g = I("n1g", [L, D]); n2g = I("n2g", [L, D])
    w_in = I("w_in", [L, D, 5120]); w_out = I("w_out", [L, D, D])
    lbl = I("lbl", [L, 1024]); hng = I("hng", [L, 128])
    pw = I("pw", [L, 4, 256, 256]); pb = I("pb", [L, 1024]); psc = I("psc", [L, 1024])
    wq = I("wq", [L, D, D]); keysT = I("keysT", [L, 128, 16, 128])
    uTb = I("uTb", [L, NK, 128, KC, 128]); pv = I("pv", [L, NE, D])
    fg = I("fg", [D]); w_adaf = I("w_adaf", [D, 2 * D]); b_adaf = I("b_adaf", [2 * D])
    cst = I("cst", [128, 1024])
    yp = O("yp", [T, D]); ys = O("ys", [NSR, D])
    nhp = O("nhp", [L, NH, 128, 128]); npp = O("npp", [L, 15, 1024])
    nhs = O("nhs", [L, NS, NH, 128, 128]); nps = O("nps", [L, NS, 15, 1024])
    S = lambda n, s, dt=F32: P.dram(n, s, dt, "Internal")
    xres = S("xres", [TT, D]); modd = S("modd", [L, NS + 1, 6 * D]); modf = S("modf", [NS + 1, 2 * D])
    mixTd = S("mixTd", [D, NSG], BF16); Wd = S("Wd", [NK, NK, NSG], BF16)

    class Arena:
        def __init__(self, name, nbytes):
            self.t = nc.alloc_sbuf_tensor(name, [128, nbytes // 4], F32)
            self.nbytes = nbytes

        def carve(self, off, shape, dt):
            esz = 4 if dt in (F32, U32) else 2
            n = int(np.prod(shape[1:]))
            assert off % 4 == 0 and off + n * esz <= self.nbytes, (off, shape, self.nbytes)
            ap = self.t[:, off // 4:(off + n * esz) // 4]
            if dt != F32:
                ap = ap.bitcast(dt)
            if len(shape) > 2:
                names = " ".join(f"d{i}" for i in range(len(shape) - 1))
                kw = {f"d{i}": shape[i + 1] for i in range(1, len(shape) - 1)}
                ap = ap.rearrange(f"p ({names}) -> p {names}", **kw)
            return ABuf(ap[0:shape[0]] if shape[0] < 128 else ap)

    class ABuf:
        def __init__(self, ap):
            self.ap0 = ap
            self.dep = Dep()
            self.subs = []

        def split(self, n):
            self.subs = [Dep() for _ in range(n)]
            return self

        def __getitem__(self, k):
            return V(self.ap0[k], [self.dep] + self.subs)

        def s(self, i):
            outer = self

            class _S:
                def __getitem__(self_, k):
                    return V(outer.ap0[k], [outer.subs[i]])
            return _S()

    KB = 1024
    RA = Arena("RA", 64 * KB)
    RB = Arena("RB", 72 * KB)
    XT = [RA.carve(i * 8 * KB, [128, D], F32) for i in range(2)]
    MA = RA.carve(16 * KB, [128, D], F32); MB = RA.carve(24 * KB, [128, D], F32)
    XT2 = RA.carve(32 * KB, [128, D], F32)
    WF = [RA.carve(32 * KB + i * 16 * KB, [128, KC, 512], BF16) for i in range(2)]
    WB = [RA.carve(i * 32 * KB, [128, KC, 1024], BF16) for i in range(2)]
    WST = RA.carve(0, [128, 128, 128], BF16)
    UC = RA.carve(0, [128, 4, KC, 128], BF16); VC = RA.carve(16 * KB, [128, 4, D], BF16)
    WC = RA.carve(32 * KB, [128, 4, NSG], BF16); AT = RA.carve(32 * KB + 8 * NSG, [128, 4, NSG], BF16)
    assert 32 * KB + 16 * NSG <= 64 * KB
    ACC = RB.carve(0, [128, NTL, D], F32)
    assert NTL * 8 * KB <= 72 * KB
    T5 = [RB.carve(i * 2 * KB, [128, 512], F32) for i in range(9)]
    XB = RB.carve(0, [128, 8, 512], F32)
    B5F = [RB.carve(18 * KB + i * 2 * KB, [128, 512], F32) for i in range(2)]
    o = 18 * KB
    VTK = RB.carve(o, [64, 16, 256], BF16); KDT = RB.carve(o + 8 * KB, [64, 16, 128], BF16)
    ATM = RB.carve(o + 12 * KB, [64, 512], BF16)
    XSL = RB.carve(o + 13 * KB, [128, 15 + 512], F32); XSS = RB.carve(o + 16 * KB, [128, 19 * NS], F32)
    PS1 = RB.carve(o + 18 * KB, [128, 15 + 512], F32); PS2 = RB.carve(o + 21 * KB, [128, 15 + 512], F32)
    RR = [RB.carve(o + 24 * KB + i * 4 * KB, [128, 1024], F32) for i in range(2)]
    DS = RB.carve(o + 32 * KB, [128, 16, 128], F32).split(16); SH = RB.carve(o + 40 * KB, [128, 16, 128], F32).split(16)
    SHB = RB.carve(o + 48 * KB, [128, 16, 128], BF16).split(16)
    SC = RB.carve(o, [128, 16, 128], F32); SCW = RB.carve(o + 8 * KB, [128, 16, 128], F32)
    OH = RB.carve(o, [128, 8, 16, 16], F32); CAW = RB.carve(o + 8 * KB, [128, 8, 256], F32)
    CA = RB.carve(o + 16 * KB, [128, 8, 256], F32)
    o2 = o + 24 * KB
    V1 = RB.carve(o2, [128, 16, 16], F32); I1U = RB.carve(o2 + KB, [128, 16, 16], U32); I1F = RB.carve(o2 + 2 * KB, [128, 16, 16], F32)
    sm = lambda i: RB.carve(o2 + 3 * KB + i * 512, [128, 8, 16], F32)
    TS_ = sm(0); K1F = sm(1); K2F = sm(2); GT = sm(3); IO1 = sm(4); IO2 = sm(5)
    PO = RB.carve(o2 + 6 * KB, [128, 8, 16], U32); POT = RB.carve(o2 + 6 * KB + 512, [128, 8, 16], U32)
    KT = RB.carve(o2 + 7 * KB, [128, 16, 128], F32)
    for b_ in (V1, I1U, SCW):
        b_.split(16)
    for b_ in (TS_, PO, CAW):
        b_.split(8)
    RT = [RB.carve(o2 + 15 * KB + i * 4 * NSG, [128, NSG], F32) for i in range(3)]
    assert o2 + 15 * KB + 12 * NSG <= 72 * KB, (o2 + 15 * KB + 12 * NSG)
    CST = P.sb("CST", [128, 1024], F32)
    ident = CST[:, 0:128]; iota = CST[:, 128:256]; tri = CST[0:64, 256:320]
    ones = CST[:, 320:448]; rcnt = CST[:, 448:464]; iota16 = CST[:, 128:144]
    mask64 = CST[:, 512:1024]
    HT = P.sb("HT", [128, KC, NSG], BF16)
    B5 = [P.sb(f"B5_{i}", [128, 512], BF16) for i in range(6)]
    GE = [P.sb(f"GE{i}", [128, 512], F32) for i in range(2)]
    SM = P.sb("SM", [128, 64], F32)
    LBT = P.sb("LBT", [128, 4, L, NH], F32)
    HNG = P.sb("HNG", [128, L], F32)
    PBS = P.sb("PBS", [128, 2, L, 8], F32)
    PWB = P.sb("PWB", [128, 8, 256], BF16)
    S32 = [P.sb(f"S32_{h}", [128, 128], F32) for h in range(NH)]
    SBF = [P.sb(f"SBF_{h}", [128, 128], BF16) for h in range(NH)]
    CAR = P.sb("CAR", [128, 8, 15], F32)
    OA = [P.sb(f"OA{i}", [128, 128], BF16) for i in range(6)]
    OB = [P.sb(f"OB{i}", [128, 128], BF16) for i in range(6)]
    OAB = [P.sb(f"OAB{i}", [128, 128], F32) for i in range(3)]
    SCT = P.sb("SCT", [128, KC, NS + 1], BF16)
    PSQ = [nc.alloc_psum_tensor(f"PSQ{i}", [128, 2048], F32) for i in range(2)]
    PSB = [ABuf(PSQ[i // 4][:, (i % 4) * 512:(i % 4 + 1) * 512]) for i in range(8)]
    PSQV = [ABuf(PSQ[i][:, :]) for i in range(2)]
    psi = [0]

    def psum():
        psi[0] = (psi[0] + 1) % 6
        return PSB[psi[0]]

    P.dma("sync", CST[:, :], cst[:, :])
    dbgn = [0]

    def dump(v, rows=128, ncol=512, dt=F32, name=None):
        if not dbg:
            return
        d = P.dram(name or f"dbg{dbgn[0]}", [rows, ncol], dt, "ExternalOutput")
        dbgn[0] += 1
        P.dma("sync", d[:, :], v)
        P.barrier()

    C17 = XT[0]
    P.dma("sync", C17[0:NS + 1, :], c17[:, :])
    P.act(XT[1][0:NS + 1, :], C17[0:NS + 1, :], AF.Silu)
    for k4 in range(4):
        pt = psum()
        for j in range(4):
            k = k4 * 4 + j
            P.tr(pt[0:128, j * 32:j * 32 + NS + 1], XT[1][0:NS + 1, k * 128:(k + 1) * 128], ident[0:NS + 1, 0:NS + 1])
        P.copy(SCT[:, k4 * 4:k4 * 4 + 4, :], pt[:, 0:128].re("p (j c) -> p j c", c=32)[:, :, 0:NS + 1], eng="act")

    def ada(wsrc, bsrc, dst, ncol):
        for n in range(ncol // 512):
            wf = WF[n % 2]
            P.dma("pool", wf[:, :, :], V(wsrc.ap[:, n * 512:(n + 1) * 512].rearrange("(k p) c -> p k c", p=128), []))
            bt = T5[n % 2]
            P.dma("sync", bt[0:NS + 1, 0:512], V(bsrc.ap[n * 512:(n + 1) * 512].partition_broadcast(NS + 1), []))
            pt = psum()
            for k in range(KC):
                P.mm(pt[0:NS + 1, 0:512], SCT[:, k, :], wf[:, k, :], start=(k == 0), stop=(k == KC - 1))
            ot = T5[2 + n % 2]
            P.tt(ot[0:NS + 1, 0:512], pt[0:NS + 1, 0:512], bt[0:NS + 1, 0:512], ALU.add)
            P.dma("sync", dst[:, n * 512:(n + 1) * 512], ot[0:NS + 1, 0:512])

    for l in range(L):
        ada(w_ada[l], b_ada[l], V(modd.t.ap()[l], [modd.dep]), 6 * D)
    ada(w_adaf[:, :], b_adaf[:], modf[:, :], 2 * D)

    P.dma("sync", LBT[:, 0, :, :], V(lbl.t.ap().rearrange("l (h p) -> p l h", p=128), []))
    P.act(LBT[:, 0, :, :], LBT[:, 0, :, :], AF.Exp)
    P.copy(LBT[:, 3, 0, :], LBT[:, 0, 0, :])
    for l in range(1, L):
        P.tt(LBT[:, 3, 0, :], LBT[:, 3, 0, :], LBT[:, 0, l, :], ALU.add)
    P.recip(LBT[:, 3, 0, :], LBT[:, 3, 0, :])
    P.memset(LBT[:, 1, 0, :], 0.0)
    for l in range(1, L):
        P.tt(LBT[:, 1, l, :], LBT[:, 0, l, :], LBT[:, 3, 0, :], ALU.mult)
        if l > 1:
            P.tt(LBT[:, 1, l, :], LBT[:, 1, l, :], LBT[:, 1, l - 1, :], ALU.add)
    for l in range(L):
        P.ts(LBT[:, 2, l, :], LBT[:, 1, l, :], -1.0, 1.0, ALU.mult, ALU.add)
    P.dma("sync", HNG[:, :], V(hng.t.ap().rearrange("l p -> p l"), []))
    P.dma("sync", PBS[:, 0, :, :], V(pb.t.ap().rearrange("l (j p) -> p l j", p=128), []))
    P.dma("sync", PBS[:, 1, :, :], V(psc.t.ap().rearrange("l (j p) -> p l j", p=128), []))
    P.barrier()

    def xkeys(kind, t0):
        return xres.r(*[(kind, t0, n) for n in range(4)]).deps

    def x_src(l, kind, t0, rows):
        if l == 0:
            return (xp[t0:t0 + rows, :] if kind == "p" else xs[0:rows, :])
        base = t0 if kind == "p" else T
        return V(xres.t.ap()[base:base + rows, :], xkeys(kind, t0))

    def x_dst(kind, t0, rows):
        base = t0 if kind == "p" else T
        return V(xres.t.ap()[base:base + rows, :], xkeys(kind, t0))

    def load_mod(dst, src2d, kind, j, rows, c0=0, c1=D):
        w = c1 - c0
        if kind == "p":
            P.dma("sync", dst[0:rows, 0:w], V(src2d.ap[0, j * D + c0:j * D + c1].partition_broadcast(rows), src2d.deps))
        else:
            for t in range(4):
                P.dma("sync", dst[t * NS:(t + 1) * NS, 0:w], V(src2d.ap[1:NS + 1, j * D + c0:j * D + c1], src2d.deps))

    def load_gain(dst, g1d, rows):
        P.dma("act", dst[0:rows, :], V(g1d.partition_broadcast(rows), []))

    cur_mod = [None]

    def prep_norm_mod(key, src2d, kind, jsh, jsc, g1d, rows):
        if cur_mod[0] == key:
            return
        cur_mod[0] = key
        load_mod(MA, src2d, kind, jsc, rows)
        load_gain(XT[1], g1d, rows)
        load_mod(MB, src2d, kind, jsh, rows)
        P.stt(MA[0:rows, :], MA[0:rows, :], 1.0, XT[1][0:rows, :], ALU.add, ALU.mult)

    def norm_mod(xt, rows, out):
        sq = XT[1]
        P.act(sq[0:rows, :], xt[0:rows, :], AF.Square)
        P.red(SM[0:rows, 0:1], sq[0:rows, :])
        P.act(SM[0:rows, 1:2], SM[0:rows, 0:1], AF.Sqrt, scale=1.0 / D, bias=EPS)
        P.recip(SM[0:rows, 2:3], SM[0:rows, 1:2])
        P.stt(out[0:rows, :], xt[0:rows, :], SM[0:rows, 2:3], MA[0:rows, :], ALU.mult, ALU.mult)
        P.tt(out[0:rows, :], out[0:rows, :], MB[0:rows, :], ALU.add)

    def to_hT(h, rows, c0):
        for k4 in range(4):
            pt = psum()
            for j in range(4):
                k = k4 * 4 + j
                P.tr(pt[0:128, j * 128:j * 128 + rows], h[0:rows, k * 128:(k + 1) * 128], ident[0:rows, 0:rows])
            src = pt[:, :].re("p (j c) -> p j c", c=128)[:, :, 0:rows]
            P.copy(HT[:, k4 * 4:k4 * 4 + 4, c0:c0 + rows], src, eng=("act" if k4 % 2 else "dve"))

    def load_w(slot, src2d, c0, ncol, dcol):
        P.dma("pool", WB[slot][:, :, dcol:dcol + ncol],
              V(src2d.ap[:, c0:c0 + ncol].rearrange("(k p) c -> p k c", p=128), []))

    def sg_cols(sg):
        cols = []
        c = 0
        for kind, t0, rows in sg:
            cols.append(c)
            c += rows
        return cols, c

    def slices_of(sg):
        out = []
        cols, _ = sg_cols(sg)
        i = 0
        while i < len(sg):
            kind = sg[i][0]
            if kind == "s":
                out.append(("s", cols[i], sg[i][2], sg[i][1]))
                i += 1
            else:
                j = i
                n = 0
                while j < len(sg) and sg[j][0] == "p" and n < 512:
                    n += sg[j][2]
                    j += 1
                out.append(("p", cols[i], n, sg[i][1]))
                i = j
        return out

    wslot = [0]

    for l in range(L):
        mod2d = V(modd.t.ap()[l], [modd.dep])
        P.dma("pool", PWB[:, :, :], V(pw.t.ap()[l].rearrange("g (ic p) o -> p (g ic) o", p=128), []))
        for h in range(NH):
            P.memset(S32[h][:, :], 0.0)
            P.memset(SBF[h][:, :], 0.0, eng="pool")
        P.memset(CAR[:, :, :], 0.0)
        for gi, sg in enumerate(sgs):
            cols, ntok = sg_cols(sg)
            slcs = slices_of(sg)
            last_sg = (gi == len(sgs) - 1)
            P.barrier()
            for ti, (kind, t0, rows) in enumerate(sg):
                prep_norm_mod((l, 1, kind), mod2d, kind, 0, 1, n1g.t.ap()[l], rows)
                xt = XT[0] if ti % 2 == 0 else XT2
                P.dma("sync", xt[0:rows, :], x_src(l, kind, t0, rows))
                norm_mod(xt, rows, xt)
                to_hT(xt, rows, cols[ti])
            P.barrier()
            for hp in range(4):
                slot = wslot[0] % 2
                wslot[0] += 1
                for ty in range(4):
                    load_w(slot, w_in[l], ty * 1024 + hp * 256, 256, ty * 256)
                W = WB[slot]
                for (kind, c0, ncol, tok0) in slcs:
                    C = 32 if kind == "p" else 4
                    nch = ncol // C if kind == "p" else NS

                    def chcols(ci, base=c0, kind=kind):
                        if kind == "p":
                            return slice(base + ci * 32, base + ci * 32 + 32)
                        return slice(base + ci, base + ci + 3 * NS + 1, NS)

                    def lcols(ci, kind=kind):
                        if kind == "p":
                            return slice(ci * 32, ci * 32 + 32)
                        return slice(ci, ci + 3 * NS + 1, NS)
                    for hv in range(2):
                        pv_ = psum()
                        for k in range(KC):
                            P.mm(pv_[:, 0:ncol], W[:, k, 512 + hv * 128:512 + hv * 128 + 128], HT[:, k, c0:c0 + ncol],
                                 start=(k == 0), stop=(k == KC - 1))
                        vT = T5[8]
                        P.act(vT[:, 0:ncol], pv_[:, 0:ncol], AF.Copy)
                        for ci in range(nch):
                            if ci % 4 == 0:
                                pt = psum()
                            P.tr(pt[0:C, (ci % 4) * 128:(ci % 4) * 128 + 128], vT[:, lcols(ci)], ident)
                            if ci % 4 == 3 or ci == nch - 1:
                                n4 = ci % 4 + 1
                                P.copy(VTK[0:C, ci - n4 + 1:ci + 1, hv * 128:hv * 128 + 128],
                                       pt[0:C, 0:n4 * 128].re("p (j c) -> p j c", c=128), eng=("act" if (ci // 4) % 2 else "dve"))
                    for hh in range(2):
                        h = hp * 2 + hh
                        zq = psum(); zf = psum(); zg = psum()
                        for (z, ty) in ((zq, 0), (zf, 1), (zg, 3)):
                            for k in range(KC):
                                P.mm(z[:, 0:ncol], W[:, k, ty * 256 + hh * 128:ty * 256 + hh * 128 + 128],
                                     HT[:, k, c0:c0 + ncol], start=(k == 0), stop=(k == KC - 1))
                        q = T5[0]; sgl = T5[1]; s1 = T5[2]; f = T5[2]; s2 = T5[3]; G = T5[4]; eG = T5[5]; emG = T5[6]
                        kt32 = T5[6]; kd = T5[4]
                        qt = B5[0]; kt = B5[1]
                        N = slice(0, ncol)
                        P.act(q[:, N], zq[:, N], AF.Silu)
                        P.act(sgl[:, N], zg[:, N], AF.Silu)
                        P.act(s1[:, N], zf[:, N], AF.Sigmoid)
                        P.act(s2[:, N], zf[:, N], AF.Sigmoid, scale=-1.0)
                        P.ts(f[:, N], s1[:, N], LBT[:, 2, l, h:h + 1], LBT[:, 1, l, h:h + 1], ALU.mult, ALU.add)
                        P.ts(s2[:, N], s2[:, N], LBT[:, 2, l, h:h + 1], None, ALU.mult)
                        P.act(f[:, N], f[:, N], AF.Ln)
                        if kind == "p":
                            P.op("dve", lambda e, G=G, f=f, N=N: e.tensor_tensor_scan(
                                out=G[:, N].ap, data0=mask64[:, N].ap, data1=f[:, N].ap, initial=0.0,
                                op0=ALU.mult, op1=ALU.add), [G[:, N]], [mask64[:, N], f[:, N]])
                        else:
                            P.copy(G[:, 0:NS], f[:, 0:NS])
                            for t in range(1, 4):
                                P.tt(G[:, t * NS:(t + 1) * NS], G[:, (t - 1) * NS:t * NS], f[:, t * NS:(t + 1) * NS], ALU.add)
                        P.act(eG[:, N], G[:, N], AF.Exp)
                        P.act(emG[:, N], G[:, N], AF.Exp, scale=-1.0)
                        P.tt(qt[:, N], q[:, N], eG[:, N], ALU.mult)
                        P.tt(kt32[:, N], s2[:, N], emG[:, N], ALU.mult)
                        P.copy(kt[:, N], kt32[:, N], eng="pool")
                        if kind == "p":
                            egc = eG[:, N].re("p (c t) -> p c t", t=32)[:, :, 31:32]
                            P.tt(kd[:, N].re("p (c t) -> p c t", t=32), kt32[:, N].re("p (c t) -> p c t", t=32),
                                 egc.bc([128, nch, 32]), ALU.mult)
                        else:
                            egc = eG[:, 3 * NS:4 * NS]
                            P.tt(kd[:, N].re("p (t s) -> p t s", s=NS), kt32[:, N].re("p (t s) -> p t s", s=NS),
                                 egc.unsq(1).bc([128, 4, NS]), ALU.mult)
                        if l == 0 and gi == 0 and h == 0 and kind == "p" and c0 == 0:
                            dump(f[:, 0:128], 128, 128); dump(G[:, 0:128], 128, 128) if False else None
                            dump(eG[:, 0:128], 128, 128); dump(kt32[:, 0:128], 128, 128); dump(kd[:, 0:128], 128, 128)
                        for ci in range(nch):
                            if ci % 4 == 0:
                                pt = psum()
                            P.tr(pt[0:C, (ci % 4) * 128:(ci % 4) * 128 + 128], kd[:, lcols(ci)], ident)
                            if ci % 4 == 3 or ci == nch - 1:
                                n4 = ci % 4 + 1
                                P.copy(KDT[0:C, ci - n4 + 1:ci + 1, :], pt[0:C, 0:n4 * 128].re("p (j c) -> p j c", c=128),
                                       eng="act")
                        pa = psum()
                        for ci in range(nch):
                            P.mm(pa[0:C, ci * C:(ci + 1) * C], kt[:, lcols(ci)], qt[:, lcols(ci)])
                        P.tt(ATM[0:C, 0:nch * C].re("p (c t) -> p c t", t=C),
                             pa[0:C, 0:nch * C].re("p (c t) -> p c t", t=C),
                             tri[0:C, 0:C].unsq(1).bc([C, nch, C]), ALU.mult)
                        atm = ATM[0:C, :]
                        po = PSB[6]; po2 = PSB[7]
                        hc = slice(hh * 128, hh * 128 + 128)
                        if kind == "p":
                            for ci in range(nch):
                                pd = psum()
                                P.mm(pd[:, 0:128], KDT[0:C, ci, :], VTK[0:C, ci, hc])
                                P.copy(DS.s(ci)[:, ci, :], pd[:, 0:128], eng="act")
                            for ci in range(nch):
                                prev = S32[h][:, :] if ci == 0 else SH.s(ci - 1)[:, ci - 1, :]
                                P.stt(SH.s(ci)[:, ci, :], prev, eG[:, ci * 32 + 31:ci * 32 + 32], DS.s(ci)[:, ci, :], ALU.mult, ALU.add)
                                P.copy(SHB.s(ci)[:, ci, :], SH.s(ci)[:, ci, :], eng="act")
                            for ci in range(nch):
                                sb_prev = SBF[h][:, :] if ci == 0 else SHB.s(ci - 1)[:, ci - 1, :]
                                P.mm(po[:, lcols(ci)], VTK[0:C, ci, hc], atm[:, ci * C:(ci + 1) * C])
                                P.mm(po2[:, lcols(ci)], sb_prev, qt[:, lcols(ci)])
                            P.copy(S32[h][:, :], SH.s(nch - 1)[:, nch - 1, :])
                            P.copy(SBF[h][:, :], SHB.s(nch - 1)[:, nch - 1, :], eng="pool")
                        else:
                            P.dma("sync", DS[:, 0:NS, :], V(sh_in.t.ap()[l, :, h].rearrange("s p v -> p s v"), []))
                            P.copy(SHB[:, 0:NS, :], DS[:, 0:NS, :], eng="pool")
                            for ci in range(nch):
                                P.mm(po[:, lcols(ci)], VTK[0:C, ci, hc], atm[:, ci * C:(ci + 1) * C])
                                P.mm(po2[:, lcols(ci)], SHB[:, ci, :], qt[:, lcols(ci)])
                                pd = psum()
                                P.mm(pd[:, 0:128], KDT[0:C, ci, :], VTK[0:C, ci, hc])
                                P.stt(SH.s(ci)[:, ci, :], DS[:, ci, :], eG[:, 3 * NS + ci:3 * NS + ci + 1], pd[:, 0:128], ALU.mult, ALU.add)
                            P.dma("sync", V(nhs.t.ap()[l, :, h].rearrange("s p v -> p s v"), []), SH[:, 0:NS, :])
                        if kind == "p" and last_sg and (c0 + ncol == sum(r for k_, _, r in sg if k_ == "p")):
                            P.dma("sync", nhp[l, h], S32[h][:, :])
                        if l == 0 and gi == 0 and h == 0 and kind == "p" and c0 == 0:
                            dump(S32[h][:, :], 128, 128)
                            dump(po[:, 0:128], 128, 128) if False else None
                        o32 = T5[0]; sq = T5[2]; rs = T5[3]
                        P.act(o32[:, N], po[:, N], AF.Copy)
                        P.tt(o32[:, N], o32[:, N], po2[:, N], ALU.add)
                        P.act(sq[:, N], o32[:, N], AF.Square)
                        pn = psum()
                        P.mm(pn[:, N], ones, sq[:, N])
                        P.act(rs[:, N], pn[:, N], AF.Sqrt, scale=1.0 / 128, bias=EPS)
                        P.recip(rs[:, N], rs[:, N])
                        P.tt(o32[:, N], o32[:, N], rs[:, N], ALU.mult)
                        oa = B5[2]
                        P.stt(oa[:, N], o32[:, N], HNG[:, l:l + 1], sgl[:, N], ALU.mult, ALU.mult)
                        P.dma("sync", V(mixTd.t.ap()[h * 128:(h + 1) * 128, c0:c0 + ncol], mixTd.r(h).deps), oa[:, N])
            slot = wslot[0] % 2
            wslot[0] += 1
            load_w(slot, w_in[l], 4096, 1024, 0)
            W = WB[slot]
            for (kind, c0, ncol, tok0) in slcs:
                if kind == "s":
                    for j in range(15):
                        r = RR[j // 8]
                        P.dma("sync", r[(j % 8) * NS:(j % 8 + 1) * NS, :], sp_in[l, :, j, :])
                pbt = {}
                for j in range(8):
                    g = j // 2
                    w = 2 << g
                    pz = psum()
                    for k in range(KC):
                        P.mm(pz[:, 0:ncol], W[:, k, j * 128:(j + 1) * 128], HT[:, k, c0:c0 + ncol],
                             start=(k == 0), stop=(k == KC - 1))
                    if kind == "p":
                        X = XSL; pre = 15; sh0 = 1
                        P.copy(X[:, 0:15], CAR[:, j, :])
                        P.copy(X[:, 15:15 + ncol], pz[:, 0:ncol], eng="act")
                        P.copy(CAR[:, j, :], X[:, ncol:ncol + 15])
                    else:
                        X = XSS; pre = 15 * NS; sh0 = NS
                        n0 = 8 * NS
                        n1 = 7 * NS
                        pt = psum()
                        P.tr(pt[:, 0:n0], RR[0][0:n0, j * 128:(j + 1) * 128], ident[0:n0, 0:n0])
                        P.tr(pt[:, n0:n0 + n1], RR[1][0:n1, j * 128:(j + 1) * 128], ident[0:n1, 0:n1])
                        P.copy(X[:, 0:pre], pt[:, 0:pre])
                        P.copy(X[:, pre:pre + ncol], pz[:, 0:ncol], eng="act")
                    tot = pre + ncol
                    s = X
                    bufs = [PS1, PS2]
                    for i in range(g + 1):
                        shf = sh0 << i
                        lo = sh0 * ((2 << i) - 1)
                        d = bufs[i % 2]
                        P.tt(d[:, lo:tot], s[:, lo:tot], s[:, lo - shf:tot - shf], ALU.add)
                        s = d
                    pl = T5[7]
                    P.stt(pl[:, 0:ncol], s[:, pre:tot], 1.0 / w, X[:, pre:tot], ALU.mult, ALU.subtract)
                    if kind == "p" and tok0 == 0:
                        P.tt(s[:, pre:pre + w - 1], s[:, pre:pre + w - 1], rcnt[:, 0:w - 1], ALU.mult)
                        P.tt(pl[:, 0:w - 1], s[:, pre:pre + w - 1], X[:, pre:pre + w - 1], ALU.subtract)
                    pbj = B5[3 + j % 2]
                    P.copy(pbj[:, 0:ncol], pl[:, 0:ncol], eng="pool")
                    pbt[j] = pbj
                    if kind == "s":
                        pt = psum()
                        n0 = 8 * NS
                        n1 = 7 * NS
                        P.tr(pt[0:n0, 0:128], X[:, 4 * NS:4 * NS + n0], ident)
                        P.tr(pt[0:n1, 128:256], X[:, 4 * NS + n0:4 * NS + n0 + n1], ident)
                        P.copy(RR[0][0:n0, j * 128:(j + 1) * 128], pt[0:n0, 0:128], eng="act")
                        P.copy(RR[1][0:n1, j * 128:(j + 1) * 128], pt[0:n1, 128:256], eng="act")
                    elif last_sg and (c0 + ncol == sum(r for k_, _, r in sg if k_ == "p")):
                        pt = psum()
                        P.tr(pt[0:15, 0:128], CAR[:, j, :], ident)
                        P.copy(RR[0][0:15, j * 128:(j + 1) * 128], pt[0:15, 0:128], eng="act")
                    if j % 2 == 1:
                        for oc in range(2):
                            pq = psum()
                            for ic in range(2):
                                P.mm(pq[:, 0:ncol], PWB[:, g * 2 + ic, oc * 128:(oc + 1) * 128], pbt[2 * g + ic][:, 0:ncol],
                                     start=(ic == 0), stop=(ic == 1))
                            ob = B5[5] if oc else B5[2]
                            jj = g * 2 + oc
                            P.ts(ob[:, 0:ncol], pq[:, 0:ncol], PBS[:, 0, l, jj:jj + 1], PBS[:, 1, l, jj:jj + 1], ALU.add, ALU.mult)
                            P.dma("sync", V(mixTd.t.ap()[1024 + jj * 128:1024 + (jj + 1) * 128, c0:c0 + ncol],
                                            mixTd.r(8 + jj).deps), ob[:, 0:ncol])
                if kind == "s":
                    for j in range(15):
                        r = RR[j // 8]
                        P.dma("sync", nps[l, :, j, :], r[(j % 8) * NS:(j % 8 + 1) * NS, :])
                elif last_sg and (c0 + ncol == sum(r for k_, _, r in sg if k_ == "p")):
                    P.dma("sync", npp[l], RR[0][0:15, :])
            P.barrier()
            for k in range(KC):
                P.dma("sync", HT[:, k, 0:ntok], V(mixTd.t.ap()[k * 128:(k + 1) * 128, 0:ntok], mixTd.r(k).deps))
            ptiles = [(ti, t0) for ti, (kind, t0, rows) in enumerate(sg) if kind == "p"]
            stiles = [(ti, t0, rows) for ti, (kind, t0, rows) in enumerate(sg) if kind == "s"]
            npt = len(ptiles)
            pt0 = ptiles[0][1]
            xsrc_p = xp.t.ap() if l == 0 else xres.t.ap()
            for n in range(4):
                slot = wslot[0] % 2
                wslot[0] += 1
                load_w(slot, w_out[l], n * 512, 512, 0)
                W = WB[slot]
                cs = slice(n * 512, (n + 1) * 512)
                pdeps = [d for (_, t0) in ptiles for d in xres.r(("p", t0, n)).deps]
                g1p = T5[8]
                load_mod(g1p, mod2d, "p", 2, 128, n * 512, (n + 1) * 512)
                P.dma("sync", XB[:, 0:npt, :], V(xsrc_p[pt0:pt0 + 128 * npt, cs].rearrange("(j p) c -> p j c", p=128),
                                                 pdeps if l > 0 else []))
                for j, (ti, t0) in enumerate(ptiles):
                    pm = psum()
                    for k in range(KC):
                        P.mm(pm[:, :], HT[:, k, cols[ti]:cols[ti] + 128], W[:, k, 0:512], start=(k == 0), stop=(k == KC - 1))
                    xo = GE[j % 2]
                    P.tt(xo[:, :], pm[:, :], g1p[:, :], ALU.mult)
                    P.tt(XB[:, j, :], XB[:, j, :], xo[:, :], ALU.add)
                P.dma("sync", V(xres.t.ap()[pt0:pt0 + 128 * npt, cs].rearrange("(j p) c -> p j c", p=128), pdeps), XB[:, 0:npt, :])
                for (ti, t0, rows) in stiles:
                    g1s = B5F[0]; xin = B5F[1]
                    load_mod(g1s, mod2d, "s", 2, rows, n * 512, (n + 1) * 512)
                    xs_v = x_src(l, "s", t0, rows)
                    P.dma("sync", xin[0:rows, :], V(xs_v.ap[:, cs], xs_v.deps))
                    pm = psum()
                    for k in range(KC):
                        P.mm(pm[0:rows, :], HT[:, k, cols[ti]:cols[ti] + rows], W[:, k, 0:512], start=(k == 0), stop=(k == KC - 1))
                    P.tt(g1s[0:rows, :], pm[0:rows, :], g1s[0:rows, :], ALU.mult)
                    P.tt(xin[0:rows, :], xin[0:rows, :], g1s[0:rows, :], ALU.add)
                    P.dma("sync", V(xres.t.ap()[T:T + rows, cs], xres.r(("s", t0, n)).deps), xin[0:rows, :])
            P.barrier()
            for ti, (kind, t0, rows) in enumerate(sg):
                prep_norm_mod((l, 2, kind), mod2d, kind, 3, 4, n2g.t.ap()[l], rows)
                xt = XT[0] if ti % 2 == 0 else XT2
                base = t0 if kind == "p" else T
                P.dma("sync", xt[0:rows, :], x_dst(kind, t0, rows))
                norm_mod(xt, rows, xt)
                to_hT(xt, rows, cols[ti])
            P.barrier()
            wslot[0] += (wslot[0] % 2)
            P.dma("sync", KT[:, :, :], keysT[l])
            load_w(0, wq[l], 0, 1024, 0)
            load_w(1, wq[l], 1024, 1024, 0)
            wslot[0] += 2
            for ti, (kind, t0, rows) in enumerate(sg):
                R = slice(0, rows)
                for n in range(4):
                    W = WB[n // 2]
                    pqr = psum()
                    for k in range(KC):
                        P.mm(pqr[R, :], HT[:, k, cols[ti]:cols[ti] + rows], W[:, k, (n % 2) * 512:(n % 2) * 512 + 512],
                             start=(k == 0), stop=(k == KC - 1))
                    qs = T5[0]; sq = T5[1]
                    P.act(qs[R, :], pqr[R, :], AF.Copy)
                    P.act(sq[R, :], pqr[R, :], AF.Square)
                    P.red(SM[R, 8:12], sq[R, :].re("p (j c) -> p j c", c=128))
                    P.act(SM[R, 12:16], SM[R, 8:12], AF.Sqrt, scale=1.0 / 128, bias=EPS)
                    P.recip(SM[R, 16:20], SM[R, 12:16])
                    pt = psum()
                    for j in range(4):
                        P.tr(pt[:, j * 128:j * 128 + rows], qs[R, j * 128:(j + 1) * 128], ident[R, R])
                    qT = T5[2]
                    P.copy(qT[:, :], pt[:, :], eng="act")
                    psc_ = psum()
                    for j in range(4):
                        P.mm(psc_[R, j * 128:(j + 1) * 128], qT[:, j * 128:j * 128 + rows], KT[:, n * 4 + j, :])
                    P.tt(SC[R, n * 4:n * 4 + 4, :], psc_[R, :].re("p (j c) -> p j c", c=128),
                         SM[R, 16:20].unsq(2).bc([rows, 4, 128]), ALU.mult)
                for g in range(16):
                    P.op("dve", lambda e, g=g, R=R: e.max(out=V1.s(g)[R, g, 0:8].ap, in_=SC[R, g, :].ap), [V1.s(g)[R, g, 0:8]], [SC[R, g, :]])
                for g in range(16):
                    P.op("dve", lambda e, g=g, R=R: e.match_replace(out=SCW.s(g)[R, g, :].ap, in_to_replace=V1.s(g)[R, g, 0:8].ap,
                                                               in_values=SC[R, g, :].ap, imm_value=NEG),
                         [SCW.s(g)[R, g, :]], [V1.s(g)[R, g, 0:8], SC[R, g, :]])
                for g in range(16):
                    P.op("dve", lambda e, g=g, R=R: e.max(out=V1.s(g)[R, g, 8:16].ap, in_=SCW.s(g)[R, g, :].ap), [V1.s(g)[R, g, 8:16]], [SCW.s(g)[R, g, :]])
                for g in range(16):
                    P.op("dve", lambda e, g=g, R=R: e.max_index(out=I1U.s(g)[R, g, 0:8].ap, in_max=V1.s(g)[R, g, 0:8].ap, in_values=SC[R, g, :].ap),
                         [I1U.s(g)[R, g, 0:8]], [V1.s(g)[R, g, 0:8], SC[R, g, :]])
                for g in range(16):
                    P.op("dve", lambda e, g=g, R=R: e.max_index(out=I1U.s(g)[R, g, 8:16].ap, in_max=V1.s(g)[R, g, 8:16].ap, in_values=SCW.s(g)[R, g, :].ap),
                         [I1U.s(g)[R, g, 8:16]], [V1.s(g)[R, g, 8:16], SCW.s(g)[R, g, :]])
                P.copy(I1F[R, :, :], I1U[R, :, :])
                if l == 0 and gi == 0 and ti == 0:
                    dump(SC[:, :, :].re("p a b -> p (a b)"), 128, 2048, name="d_SC")
                    dump(V1[:, :, :].re("p a b -> p (a b)"), 128, 256, name="d_V1a")
                v1v = V1[R, :, :].re("p (h q) k -> p h q k", q=2)
                P.tt(CA[R, :, :].re("p h (a b) -> p h a b", b=16), v1v[:, :, 0, :].unsq(3).bc([rows, 8, 16, 16]),
                     v1v[:, :, 1, :].unsq(2).bc([rows, 8, 16, 16]), ALU.add)
                for h in range(8):
                    P.op("dve", lambda e, h=h, R=R: e.max(out=TS_.s(h)[R, h, 0:8].ap, in_=CA[R, h, :].ap), [TS_.s(h)[R, h, 0:8]], [CA[R, h, :]])
                for h in range(8):
                    P.op("dve", lambda e, h=h, R=R: e.match_replace(out=CAW.s(h)[R, h, :].ap, in_to_replace=TS_.s(h)[R, h, 0:8].ap,
                                                               in_values=CA[R, h, :].ap, imm_value=NEG),
                         [CAW.s(h)[R, h, :]], [TS_.s(h)[R, h, 0:8], CA[R, h, :]])
                for h in range(8):
                    P.op("dve", lambda e, h=h, R=R: e.max(out=TS_.s(h)[R, h, 8:16].ap, in_=CAW.s(h)[R, h, :].ap), [TS_.s(h)[R, h, 8:16]], [CAW.s(h)[R, h, :]])
                for h in range(8):
                    P.op("dve", lambda e, h=h, R=R: e.max_index(out=PO.s(h)[R, h, 0:8].ap, in_max=TS_.s(h)[R, h, 0:8].ap, in_values=CA[R, h, :].ap),
                         [PO.s(h)[R, h, 0:8]], [TS_.s(h)[R, h, 0:8], CA[R, h, :]])
                for h in range(8):
                    P.op("dve", lambda e, h=h, R=R: e.max_index(out=PO.s(h)[R, h, 8:16].ap, in_max=TS_.s(h)[R, h, 8:16].ap, in_values=CAW.s(h)[R, h, :].ap),
                         [PO.s(h)[R, h, 8:16]], [TS_.s(h)[R, h, 8:16], CAW.s(h)[R, h, :]])
                P.tt(GT[R, :, :], TS_[R, :, :], TS_[R, :, 0:1].bc([rows, 8, 16]), ALU.subtract)
                P.act(GT[R, :, :], GT[R, :, :], AF.Exp)
                P.red(SM[R, 24:32], GT[R, :, :])
                P.recip(SM[R, 32:40], SM[R, 24:32])
                P.tt(GT[R, :, :], GT[R, :, :], SM[R, 32:40].unsq(2).bc([rows, 8, 16]), ALU.mult)
                P.copy(K2F[R, :, :], PO[R, :, :])
                P.ts(SM[R, 40:56], iota16[R, :], 16.0, 16.0, ALU.mult, ALU.add)
                P.tt(OH[R, :, :, :], K2F[R, :, :].unsq(3).bc([rows, 8, 16, 16]),
                     SM[R, 40:56].unsq(1).unsq(1).bc([rows, 8, 16, 16]), ALU.is_ge)
                P.red(K1F[R, :, :], OH[R, :, :, :])
                P.stt(K2F[R, :, :], K1F[R, :, :], -16.0, K2F[R, :, :], ALU.mult, ALU.add)
                i1v = I1F[R, :, :].re("p (h q) k -> p h q k", q=2)
                for (KF, qq, IO) in ((K1F, 0, IO1), (K2F, 1, IO2)):
                    P.tt(OH[R, :, :, :], KF[R, :, :].unsq(3).bc([rows, 8, 16, 16]),
                         iota16[R, :].unsq(1).unsq(1).bc([rows, 8, 16, 16]), ALU.is_equal)
                    P.tt(OH[R, :, :, :], OH[R, :, :, :], i1v[:, :, qq, :].unsq(2).bc([rows, 8, 16, 16]), ALU.mult)
                    P.red(IO[R, :, :], OH[R, :, :, :])
                if l == 0 and gi == 0 and ti == 0:
                    dump(I1F[:, :, :].re("p a b -> p (a b)"), 128, 256, name="d_I1F")
                    dump(V1[:, :, :].re("p a b -> p (a b)"), 128, 256, name="d_V1")
                    dump(TS_[:, :, :].re("p a b -> p (a b)"), 128, 128, name="d_TS")
                    dump(PO[:, :, :].re("p a b -> p (a b)"), 128, 128, dt=U32, name="d_PO")
                    dump(K1F[:, :, :].re("p a b -> p (a b)"), 128, 128, name="d_K1F")
                    dump(K2F[:, :, :].re("p a b -> p (a b)"), 128, 128, name="d_K2F")
                    dump(IO1[:, :, :].re("p a b -> p (a b)"), 128, 128, name="d_IO1")
                    dump(IO2[:, :, :].re("p a b -> p (a b)"), 128, 128, name="d_IO2")
                    dump(GT[:, :, :].re("p a b -> p (a b)"), 128, 128, name="d_GT")
                pt = psum()
                for j, src in enumerate((IO1, IO2, GT)):
                    P.tr(pt[:, j * 128:j * 128 + rows], src[R, :, :].re("p h k -> p (h k)"), ident[R, R])
                for j in range(3):
                    P.copy(RT[j][:, cols[ti]:cols[ti] + rows], pt[:, j * 128:j * 128 + rows], eng="act")
            P.barrier()
            P.ts(RT[1][:, 0:ntok], RT[1][:, 0:ntok], -1.0, None, ALU.mult)
            for ti, (kind, t0, rows) in enumerate(sg):
                for tk0 in range(0, rows, 16):
                    qd = PSQV[(tk0 // 16) % 2]
                    for j in range(16):
                        tk = tk0 + j
                        col = cols[ti] + tk
                        oa = OA[tk % 6]; ob = OB[tk % 6]
                        P.ts(oa[:, :], iota, RT[0][:, col:col + 1], RT[2][:, col:col + 1], ALU.is_equal, ALU.mult)
                        if tk % 3 == 0:
                            ab = OAB[(tk // 3) % 3]
                            P.op("act", lambda e, ab=ab, col=col: e.activation(out=ab[:, :].ap, in_=iota.ap, func=AF.Abs, scale=1.0,
                                                                            bias=RT[1][:, col:col + 1].ap), [ab[:, :]], [iota, RT[1][:, col:col + 1]])
                            P.act(ob[:, :], ab[:, :], AF.Relu, scale=-1.0, bias=1.0)
                        else:
                            P.ts(ob[:, :], iota, -1.0, RT[1][:, col:col + 1], ALU.mult, ALU.is_equal)
                        bq, t4 = j // 4, j % 4
                        P.mm(qd[:, bq * 512 + t4:bq * 512 + 512:4], ob[:, :], oa[:, :])
                    P.copy(WST[:, :, tk0:tk0 + 16].re("p i (b t) -> p i b t", t=4),
                           qd[:, :].re("p (b i t) -> p i b t", b=4, t=4), eng="act")
                if l == 0 and gi == 0 and ti == 0:
                    for j in range(3):
                        dump(RT[j][:, 0:128], 128, 128, name=f"d_RT{j}")
                    dump(WST[:, :, 70], 128, 128, dt=BF16, name="d_W70")
                    dump(WST[:, :, 5], 128, 128, dt=BF16, name="d_W5")
                for c8 in range(8):
                    P.dma("sync", V(Wd.t.ap()[c8 * 16:(c8 + 1) * 16, :, cols[ti]:cols[ti] + rows].rearrange("c i t -> i c t"),
                                    Wd.r(c8).deps), WST[:, c8 * 16:(c8 + 1) * 16, 0:rows])
            P.barrier()
            tgs = []
            c = 0
            while c < ntok:
                n = min(512, ntok - c)
                tgs.append((c, n))
                c += n
            CG = 4
            for cg in range(NK // CG):
                P.dma("pool", UC[:, :, :, :], V(uTb.t.ap()[l, cg * CG:(cg + 1) * CG].rearrange("c p k e -> p c k e"), []))
                P.dma("pool", VC[:, :, :], V(pv.t.ap()[l, cg * CG * 128:(cg + 1) * CG * 128, :].rearrange("(c p) d -> p c d", p=128), []))
                P.dma("sync", WC[:, :, 0:ntok], V(Wd.t.ap()[cg * CG:(cg + 1) * CG, :, 0:ntok].rearrange("c i t -> i c t"),
                                                  Wd.r(cg * CG // 16).deps))
                for ci in range(CG):
                    for (tc0, tn) in tgs:
                        pu = psum()
                        for k in range(KC):
                            P.mm(pu[:, 0:tn], UC[:, ci, k, :], HT[:, k, tc0:tc0 + tn], start=(k == 0), stop=(k == KC - 1))
                        ge = GE[(ci + tc0 // 512) % 2]
                        P.act(ge[:, 0:tn], pu[:, 0:tn], AF.Gelu)
                        P.tt(AT[:, ci, tc0:tc0 + tn], ge[:, 0:tn], WC[:, ci, tc0:tc0 + tn], ALU.mult)
                for ti, (kind, t0, rows) in enumerate(sg):
                    for dq in range(4):
                        pp = psum()
                        for ci in range(CG):
                            P.mm(pp[0:rows, :], AT[:, ci, cols[ti]:cols[ti] + rows], VC[:, ci, dq * 512:(dq + 1) * 512],
                                 start=(ci == 0), stop=(ci == CG - 1))
                        a = ACC[0:rows, ti, dq * 512:(dq + 1) * 512]
                        if cg == 0:
                            P.copy(a, pp[0:rows, :])
                        else:
                            P.tt(a, a, pp[0:rows, :], ALU.add)
            P.barrier()
            gk = None
            for ti, (kind, t0, rows) in enumerate(sg):
                if gk != kind:
                    gk = kind
                    load_mod(MA, mod2d, kind, 5, rows)
                    cur_mod[0] = None
                xt = XT[0]
                base = t0 if kind == "p" else T
                P.dma("sync", xt[0:rows, :], x_dst(kind, t0, rows))
                P.tt(ACC[0:rows, ti, :], ACC[0:rows, ti, :], MA[0:rows, :], ALU.mult)
                if dbg != 2:
                    P.tt(xt[0:rows, :], xt[0:rows, :], ACC[0:rows, ti, :], ALU.add)
                if l < L - 1:
                    P.dma("sync", x_dst(kind, t0, rows), xt[0:rows, :])
                else:
                    P.dma("sync", x_dst(kind, t0, rows), xt[0:rows, :])
            if l == L - 1:
                for ti, (kind, t0, rows) in enumerate(sg):
                    prep_norm_mod(("f", kind), modf[:, :], kind, 0, 1, fg.t.ap(), rows)
                    xt = XT[0]
                    P.dma("sync", xt[0:rows, :], x_dst(kind, t0, rows))
                    norm_mod(xt, rows, xt)
                    dst = yp[t0:t0 + rows, :] if kind == "p" else ys[0:rows, :]
                    P.dma("sync", dst, xt[0:rows, :])
    P.emit()
    return nc


def host_consts():
    c = np.zeros((128, 1024), np.float32)
    c[:, 0:128] = np.eye(128, dtype=np.float32)
    c[:, 128:256] = np.arange(128, dtype=np.float32)[None, :]
    c[0:64, 256:320] = np.triu(np.ones((64, 64), np.float32))
    c[:, 320:448] = 1.0
    c[:, 448:464] = 1.0 / np.arange(1, 17, dtype=np.float32)[None, :]
    m = np.ones(512, np.float32)
    m[::32] = 0.0
    c[:, 512:1024] = m[None, :]
    return c


def make_in_maps(inp, T, NS, L, n_cores, n_prompt):
    f = lambda a: np.ascontiguousarray(np.asarray(a, dtype=np.float32))
    keysT = f(np.asarray(inp["peer_keys"]).reshape(L, 16, NK, 128).transpose(0, 3, 1, 2))
    u = np.asarray(inp["peer_u"])
    uTb = f(u.reshape(L, NK, 128, KC, 128).transpose(0, 1, 4, 3, 2))
    shared = {
        "w_ada": f(inp["w_ada"]), "b_ada": f(inp["b_ada"]), "n1g": f(inp["norm1_g"]), "n2g": f(inp["norm2_g"]),
        "w_in": f(inp["w_in"]), "w_out": f(inp["w_out"]), "lbl": f(inp["lb_logits"]), "hng": f(inp["hgrn_norm_g"]),
        "pw": f(inp["pool_w"]), "pb": f(inp["pool_b"]), "psc": f(inp["pool_scale"]), "wq": f(inp["peer_wq"]),
        "keysT": keysT, "uTb": uTb, "pv": f(inp["peer_v"]), "fg": f(inp["final_g"]),
        "w_adaf": f(inp["w_ada_final"]), "b_adaf": f(inp["b_ada_final"]), "cst": host_consts(),
    }
    maps = []
    xp_all = np.asarray(inp["x_prompt"]); xs_all = np.asarray(inp["x_sample"])
    cp = np.asarray(inp["c_prompt"]); cs = np.asarray(inp["c_sample"])
    sh = np.asarray(inp["state_hgrn"]); sp = np.asarray(inp["state_pool"])
    for c in range(n_cores):
        b = c % n_prompt
        s0 = c * NS
        m = dict(shared)
        m["xp"] = f(xp_all[b])
        m["xs"] = f(xs_all[s0:s0 + NS].transpose(1, 0, 2).reshape(4 * NS, D))
        m["c17"] = f(np.concatenate([cp[b:b + 1], cs[s0:s0 + NS]], axis=0))
        m["sh_in"] = f(sh[:, s0:s0 + NS])
        m["sp_in"] = f(sp[:, s0:s0 + NS])
        maps.append(m)
    return maps


def gather(res, T, NS, L, n_cores, n_prompt):
    y_p = np.stack([res[b]["yp"] for b in range(n_prompt)])
    y_s = np.concatenate([res[c]["ys"].reshape(4, NS, D).transpose(1, 0, 2) for c in range(n_cores)], axis=0)
    nh_p = np.stack([res[b]["nhp"] for b in range(n_prompt)], axis=1)
    np_p = np.stack([res[b]["npp"] for b in range(n_prompt)], axis=1)
    nh_s = np.concatenate([res[c]["nhs"] for c in range(n_cores)], axis=1)
    np_s = np.concatenate([res[c]["nps"] for c in range(n_cores)], axis=1)
    return tuple(np.ascontiguousarray(a, dtype=np.float32) for a in (y_p, y_s, nh_p, np_p, nh_s, np_s))


def kernel(**inputs):
    T = 2048; NS = 16; L = 2; n_cores = 8; n_prompt = 4
    nc = build(T, NS, L)
    in_maps = make_in_maps(inputs, T, NS, L, n_cores, n_prompt)
    res = run_bass_kernel_spmd(nc, in_maps, core_ids=list(range(n_cores)))
    return gather(res.results, T, NS, L, n_cores, n_prompt)
```

```python
import numpy as np
from contextlib import ExitStack
import concourse.bass as bass
import concourse.mybir as mybir
from concourse.bass_utils import run_bass_kernel_spmd

F32 = mybir.dt.float32
BF16 = mybir.dt.bfloat16
U32 = mybir.dt.uint32
AF = mybir.ActivationFunctionType
ALU = mybir.AluOpType
AX = mybir.AxisListType

D = 2048
KC = 16
NH = 8
NPH = 8
NK = 128
TOPK = 16
EPS = 1e-6
NEG = -1.0e30

ENGS = ("act", "pool", "dve", "pe")
N_DMA_SEM = 40
EPOCH = 30000
SAME_ENGINE_SYNC = True


class Dep:
    __slots__ = ("lw", "rd")

    def __init__(self):
        self.lw = None
        self.rd = {}


class V:
    __slots__ = ("ap", "deps")

    def __init__(self, ap, deps):
        self.ap = ap
        self.deps = deps

    def __getitem__(self, k):
        return V(self.ap[k], self.deps)

    def bc(self, shape):
        return V(self.ap.to_broadcast(list(shape)), self.deps)

    def re(self, pat, **kw):
        return V(self.ap.rearrange(pat, **kw), self.deps)

    def unsq(self, ax):
        return V(self.ap.unsqueeze(ax), self.deps)

    def bitcast(self, dt):
        return V(self.ap.bitcast(dt), self.deps)

    @property
    def shape(self):
        return self.ap.shape


class Buf:
    def __init__(self, t, tracked=True, is_dram=False):
        self.t = t
        self.tracked = tracked
        self.is_dram = is_dram
        self.dep = Dep()
        self.regions = {}

    def _base(self):
        return self.t.ap() if self.is_dram else self.t

    def __getitem__(self, k):
        return V(self._base()[k], [self.dep] if self.tracked else [])

    def v(self):
        return self[:]

    def r(self, *keys):
        deps = []
        for key in keys:
            if key not in self.regions:
                self.regions[key] = Dep()
            deps.append(self.regions[key])
        return V(self._base()[:], deps)


class Prog:
    def __init__(self, nc):
        self.nc = nc
        self.ops = {e: [] for e in ("sync", "act", "pool", "dve", "pe")}
        self.nsem = N_DMA_SEM
        self.esem = {}
        self.cnt = {}
        for e in ENGS:
            self.esem[e] = self.nsem
            self.nsem += 1
            self.cnt[e] = 0
        self.all_esems = {e: [self.esem[e]] for e in ENGS}
        self.dma_cnt = [0] * N_DMA_SEM
        self.dma_rr = 0
        self.waited = {e: {} for e in self.ops}
        self.pending = {e: {} for e in self.ops}

    def barrier(self):
        evs = []
        for i in range(N_DMA_SEM):
            if self.dma_cnt[i]:
                evs.append((i, 16 * self.dma_cnt[i]))
        for e in ENGS:
            if self.cnt[e]:
                evs.append((self.esem[e], self.cnt[e]))
        for e in self.ops:
            for s, v in evs:
                if self.pending[e].get(s, 0) < v:
                    self.pending[e][s] = v

    def sb(self, name, shape, dt):
        return Buf(self.nc.alloc_sbuf_tensor(name, list(shape), dt))

    def ps(self, name, shape, dt=F32):
        return Buf(self.nc.alloc_psum_tensor(name, list(shape), dt))

    def dram(self, name, shape, dt, kind):
        return Buf(self.nc.dram_tensor(name, list(shape), dt, kind=kind),
                   tracked=(kind == "Internal"), is_dram=True)

    def _record(self, eng, fn, outs, ins, is_dma):
        waits = dict(self.pending[eng])
        self.pending[eng] = {}

        def need(ev):
            if ev is None:
                return
            s, v = ev
            if waits.get(s, 0) < v:
                waits[s] = v

        for x in ins:
            for d in x.deps:
                need(d.lw)
        for x in outs:
            for d in x.deps:
                need(d.lw)
                for s, v in d.rd.items():
                    need((s, v))
        if is_dma:
            sid = self.dma_rr
            self.dma_rr = (self.dma_rr + 1) % N_DMA_SEM
            if self.dma_cnt[sid] > 0:
                need((sid, 16 * self.dma_cnt[sid]))
            self.dma_cnt[sid] += 1
            ev = (sid, 16 * self.dma_cnt[sid])
            inc = (sid, 16)
        else:
            if self.cnt[eng] >= EPOCH:
                self.esem[eng] = self.nsem
                self.all_esems[eng].append(self.nsem)
                self.nsem += 1
                self.cnt[eng] = 0
            sid = self.esem[eng]
            self.cnt[eng] += 1
            ev = (sid, self.cnt[eng])
            inc = (sid, 1)
        wl = []
        wd = self.waited[eng]
        for s, v in waits.items():
            if (not is_dma) and s in self.all_esems.get(eng, ()):
                if eng == "pe" or not SAME_ENGINE_SYNC:
                    continue
                if s != sid or (ev[1] - v) >= 3:
                    continue
            if wd.get(s, 0) >= v:
                continue
            wd[s] = v
            wl.append((s, v))
        self.ops[eng].append((wl, fn, inc))
        for x in ins:
            for d in x.deps:
                if d.rd.get(ev[0], 0) < ev[1]:
                    d.rd[ev[0]] = ev[1]
        for x in outs:
            for d in x.deps:
                d.lw = ev
                d.rd = {}
        return ev

    def dma(self, q, out, in_):
        return self._record(q, lambda e: e.dma_start(out=out.ap, in_=in_.ap), [out], [in_], True)

    def act(self, out, in_, func, scale=1.0, bias=0.0, extra=()):
        def fn(e):
            return e.activation(out=out.ap, in_=in_.ap, func=func, scale=scale, bias=bias)
        return self._record("act", fn, [out], [in_, *extra], False)

    def op(self, eng, fn, outs, ins):
        return self._record(eng, fn, outs, ins, False)

    def tt(self, out, a, b, op, eng="dve"):
        return self.op(eng, lambda e: e.tensor_tensor(out=out.ap, in0=a.ap, in1=b.ap, op=op), [out], [a, b])

    def ts(self, out, a, s1, s2, op0, op1=None, eng="dve"):
        ins = [a] + [s for s in (s1, s2) if isinstance(s, V)]
        a1 = s1.ap if isinstance(s1, V) else s1
        a2 = s2.ap if isinstance(s2, V) else s2

        def fn(e):
            if op1 is None:
                return e.tensor_scalar(out=out.ap, in0=a.ap, scalar1=a1, scalar2=None, op0=op0)
            return e.tensor_scalar(out=out.ap, in0=a.ap, scalar1=a1, scalar2=a2, op0=op0, op1=op1)
        return self.op(eng, fn, [out], ins)

    def stt(self, out, a, s, b, op0, op1):
        ins = [a, b] + ([s] if isinstance(s, V) else [])
        sa = s.ap if isinstance(s, V) else s
        return self.op("dve", lambda e: e.scalar_tensor_tensor(out=out.ap, in0=a.ap, scalar=sa, in1=b.ap,
                                                               op0=op0, op1=op1), [out], ins)

    def copy(self, out, in_, eng="dve"):
        if eng == "act":
            return self.act(out, in_, AF.Copy)
        return self.op(eng, lambda e: e.tensor_copy(out=out.ap, in_=in_.ap), [out], [in_])

    def memset(self, out, val, eng="dve"):
        return self.op(eng, lambda e: e.memset(out.ap, val), [out], [])

    def red(self, out, in_, op=ALU.add, axis=AX.X):
        return self.op("dve", lambda e: e.tensor_reduce(out=out.ap, in_=in_.ap, axis=axis, op=op), [out], [in_])

    def recip(self, out, in_):
        return self.op("dve", lambda e: e.reciprocal(out=out.ap, in_=in_.ap), [out], [in_])

    def mm(self, out, lhsT, rhs, start=True, stop=True):
        return self.op("pe", lambda e: e.matmul(out.ap, lhsT.ap, rhs.ap, start=start, stop=stop),
                       [out], [lhsT, rhs])

    def tr(self, out, in_, ident):
        return self.op("pe", lambda e: e.transpose(out.ap, in_.ap, ident.ap), [out], [in_, ident])

    def emit(self):
        nc = self.nc
        fin = []
        for i in range(N_DMA_SEM):
            if self.dma_cnt[i]:
                fin.append((i, 16 * self.dma_cnt[i]))
        for e in ENGS:
            if self.cnt[e]:
                fin.append((self.esem[e], self.cnt[e]))
        with ExitStack() as es:
            es.enter_context(nc.allow_non_contiguous_dma(reason="small strided parameter loads"))
            sems = [es.enter_context(nc.semaphore(f"s{i}")) for i in range(self.nsem)]
            block = es.enter_context(nc.Block())

            def run(eng_name, e):
                for wl, fn, inc in self.ops[eng_name]:
                    for s, v in wl:
                        e.wait_ge(sems[s], v)
                    ins = fn(e)
                    ins.then_inc(sems[inc[0]], inc[1])
                if eng_name == "sync":
                    for s, v in fin:
                        e.wait_ge(sems[s], v)
                    e.nop()

            @block.sync
            def _(e):
                run("sync", e)

            @block.scalar
            def _(e):
                run("act", e)

            @block.gpsimd
            def _(e):
                run("pool", e)

            @block.vector
            def _(e):
                run("dve", e)

            @block.tensor
            def _(e):
                run("pe", e)


def build(T, NS, L, dbg=False):
    nc = bass.Bass("TRN2", target_bir_lowering=False)
    P = Prog(nc)
    NSR = 4 * NS
    TT = T + NSR
    NE = NK * NK
    half = T // 2 if T >= 256 else T
    sgs = []
    if T >= 256:
        sgs.append([("p", t0, 128) for t0 in range(0, half, 128)])
        sgs.append([("p", t0, 128) for t0 in range(half, T, 128)] + [("s", 0, NSR)])
    else:
        sgs.append([("p", t0, 128) for t0 in range(0, T, 128)] + [("s", 0, NSR)])
    NSG = max(sum(r for _, _, r in sg) for sg in sgs)
    NTL = max(len(sg) for sg in sgs)

    I = lambda n, s, dt=F32: P.dram(n, s, dt, "ExternalInput")
    O = lambda n, s: P.dram(n, s, F32, "ExternalOutput")
    xp = I("xp", [T, D]); xs = I("xs", [NSR, D]); c17 = I("c17", [NS + 1, D])
    sh_in = I("sh_in", [L, NS, NH, 128, 128]); sp_in = I("sp_in", [L, NS, 15, 1024])
    w_ada = I("w_ada", [L, D, 6 * D]); b_ada = I("b_ada", [L, 6 * D])
    n1g = I("n1g", [L, D]); n2g = I("n2g", [L, D])
    w_in = I("w_in", [L, D, 5120]); w_out = I("w_out", [L, D, D])
    lbl = I("lbl", [L, 1024]); hng = I("hng", [L, 128])
    pw = I("pw", [L, 4, 256, 256]); pb = I("pb", [L, 1024]); psc = I("psc", [L, 1024])
    wq = I("wq", [L, D, D]); keysT = I("keysT", [L, 128, 16, 128])
    uTb = I("uTb", [L, NK, 128, KC, 128]); pv = I("pv", [L, NE, D])
    fg = I("fg", [D]); w_adaf = I("w_adaf", [D, 2 * D]); b_adaf = I("b_adaf", [2 * D])
    cst = I("cst", [128, 1024])
    yp = O("yp", [T, D]); ys = O("ys", [NSR, D])
    nhp = O("nhp", [L, NH, 128, 128]); npp = O("npp", [L, 15, 1024])
    nhs = O("nhs", [L, NS, NH, 128, 128]); nps = O("nps", [L, NS, 15, 1024])
    S = lambda n, s, dt=F32: P.dram(n, s, dt, "Internal")
    xres = S("xres", [TT, D]); modd = S("modd", [L, NS + 1, 6 * D]); modf = S("modf", [NS + 1, 2 * D])
    mixTd = S("mixTd", [D, NSG], BF16); Wd = S("Wd", [NK, NK, NSG], BF16)

    class Arena:
        def __init__(self, name, nbytes):
            self.t = nc.alloc_sbuf_tensor(name, [128, nbytes // 4], F32)
            self.nbytes = nbytes

        def carve(self, off, shape, dt):
            esz = 4 if dt in (F32, U32) else 2
            n = int(np.prod(shape[1:]))
            assert off % 4 == 0 and off + n * esz <= self.nbytes, (off, shape, self.nbytes)
            ap = self.t[:, off // 4:(off + n * esz) // 4]
            if dt != F32:
                ap = ap.bitcast(dt)
            if len(shape) > 2:
                names = " ".join(f"d{i}" for i in range(len(shape) - 1))
                kw = {f"d{i}": shape[i + 1] for i in range(1, len(shape) - 1)}
                ap = ap.rearrange(f"p ({names}) -> p {names}", **kw)
            return ABuf(ap[0:shape[0]] if shape[0] < 128 else ap)

    class ABuf:
        def __init__(self, ap):
            self.ap0 = ap
            self.dep = Dep()
            self.subs = []

        def split(self, n):
            self.subs = [Dep() for _ in range(n)]
            return self

        def __getitem__(self, k):
            return V(self.ap0[k], [self.dep] + self.subs)

        def s(self, i):
            outer = self

            class _S:
                def __getitem__(self_, k):
                    return V(outer.ap0[k], [outer.subs[i]])
            return _S()

    KB = 1024
    RA = Arena("RA", 64 * KB)
    RB = Arena("RB", 72 * KB)
    XT = [RA.carve(i * 8 * KB, [128, D], F32) for i in range(2)]
    MA = RA.carve(16 * KB, [128, D], F32); MB = RA.carve(24 * KB, [128, D], F32)
    XT2 = RA.carve(32 * KB, [128, D], F32)
    WF = [RA.carve(32 * KB + i * 16 * KB, [128, KC, 512], BF16) for i in range(2)]
    WB = [RA.carve(i * 32 * KB, [128, KC, 1024], BF16) for i in range(2)]
    WST = RA.carve(0, [128, 128, 128], BF16)
    UC = RA.carve(0, [128, 4, KC, 128], BF16); VC = RA.carve(16 * KB, [128, 4, D], BF16)
    WC = RA.carve(32 * KB, [128, 4, NSG], BF16); AT = RA.carve(32 * KB + 8 * NSG, [128, 4, NSG], BF16)
    assert 32 * KB + 16 * NSG <= 64 * KB
    ACC = RB.carve(0, [128, NTL, D], F32)
    assert NTL * 8 * KB <= 72 * KB
    T5 = [RB.carve(i * 2 * KB, [128, 512], F32) for i in range(9)]
    XB = RB.carve(0, [128, 8, 512], F32)
    B5F = [RB.carve(18 * KB + i * 2 * KB, [128, 512], F32) for i in range(2)]
    o = 18 * KB
    VTK = RB.carve(o, [64, 16, 256], BF16); KDT = RB.carve(o + 8 * KB, [64, 16, 128], BF16)
    ATM = RB.carve(o + 12 * KB, [64, 512], BF16)
    XSL = RB.carve(o + 13 * KB, [128, 15 + 512], F32); XSS = RB.carve(o + 16 * KB, [128, 19 * NS], F32)
    PS1 = RB.carve(o + 18 * KB, [128, 15 + 512], F32); PS2 = RB.carve(o + 21 * KB, [128, 15 + 512], F32)
    RR = [RB.carve(o + 24 * KB + i * 4 * KB, [128, 1024], F32) for i in range(2)]
    DS = RB.carve(o + 32 * KB, [128, 16, 128], F32).split(16); SH = RB.carve(o + 40 * KB, [128, 16, 128], F32).split(16)
    SHB = RB.carve(o + 48 * KB, [128, 16, 128], BF16).split(16)
    SC = RB.carve(o, [128, 16, 128], F32); SCW = RB.carve(o + 8 * KB, [128, 16, 128], F32)
    OH = RB.carve(o, [128, 8, 16, 16], F32); CAW = RB.carve(o + 8 * KB, [128, 8, 256], F32)
    CA = RB.carve(o + 16 * KB, [128, 8, 256], F32)
    o2 = o + 24 * KB
    V1 = RB.carve(o2, [128, 16, 16], F32); I1U = RB.carve(o2 + KB, [128, 16, 16], U32); I1F = RB.carve(o2 + 2 * KB, [128, 16, 16], F32)
    sm = lambda i: RB.carve(o2 + 3 * KB + i * 512, [128, 8, 16], F32)
    TS_ = sm(0); K1F = sm(1); K2F = sm(2); GT = sm(3); IO1 = sm(4); IO2 = sm(5)
    PO = RB.carve(o2 + 6 * KB, [128, 8, 16], U32); POT = RB.carve(o2 + 6 * KB + 512, [128, 8, 16], U32)
    KT = RB.carve(o2 + 7 * KB, [128, 16, 128], F32)
    for b_ in (V1, I1U, SCW):
        b_.split(16)
    for b_ in (TS_, PO, CAW):
        b_.split(8)
    RT = [RB.carve(o2 + 15 * KB + i * 4 * NSG, [128, NSG], F32) for i in range(3)]
    assert o2 + 15 * KB + 12 * NSG <= 72 * KB, (o2 + 15 * KB + 12 * NSG)
    CST = P.sb("CST", [128, 1024], F32)
    ident = CST[:, 0:128]; iota = CST[:, 128:256]; tri = CST[0:64, 256:320]
    ones = CST[:, 320:448]; rcnt = CST[:, 448:464]; iota16 = CST[:, 128:144]
    mask64 = CST[:, 512:1024]
    HT = P.sb("HT", [128, KC, NSG], BF16)
    B5 = [P.sb(f"B5_{i}", [128, 512], BF16) for i in range(6)]
    GE = [P.sb(f"GE{i}", [128, 512], F32) for i in range(2)]
    SM = P.sb("SM", [128, 64], F32)
    LBT = P.sb("LBT", [128, 4, L, NH], F32)
    HNG = P.sb("HNG", [128, L], F32)
    PBS = P.sb("PBS", [128, 2, L, 8], F32)
    PWB = P.sb("PWB", [128, 8, 256], BF16)
    S32 = [P.sb(f"S32_{h}", [128, 128], F32) for h in range(NH)]
    SBF = [P.sb(f"SBF_{h}", [128, 128], BF16) for h in range(NH)]
    CAR = P.sb("CAR", [128, 8, 15], F32)
    OA = [P.sb(f"OA{i}", [128, 128], BF16) for i in range(6)]
    OB = [P.sb(f"OB{i}", [128, 128], BF16) for i in range(6)]
    OAB = [P.sb(f"OAB{i}", [128, 128], F32) for i in range(3)]
    SCT = P.sb("SCT", [128, KC, NS + 1], BF16)
    PSQ = [nc.alloc_psum_tensor(f"PSQ{i}", [128, 2048], F32) for i in range(2)]
    PSB = [ABuf(PSQ[i // 4][:, (i % 4) * 512:(i % 4 + 1) * 512]) for i in range(8)]
    PSQV = [ABuf(PSQ[i][:, :]) for i in range(2)]
    psi = [0]

    def psum():
        psi[0] = (psi[0] + 1) % 6
        return PSB[psi[0]]

    P.dma("sync", CST[:, :], cst[:, :])
    dbgn = [0]

    def dump(v, rows=128, ncol=512, dt=F32, name=None):
        if not dbg:
            return
        d = P.dram(name or f"dbg{dbgn[0]}", [rows, ncol], dt, "ExternalOutput")
        dbgn[0] += 1
        P.dma("sync", d[:, :], v)
        P.barrier()

    C17 = XT[0]
    P.dma("sync", C17[0:NS + 1, :], c17[:, :])
    P.act(XT[1][0:NS + 1, :], C17[0:NS + 1, :], AF.Silu)
    for k4 in range(4):
        pt = psum()
        for j in range(4):
            k = k4 * 4 + j
            P.tr(pt[0:128, j * 32:j * 32 + NS + 1], XT[1][0:NS + 1, k * 128:(k + 1) * 128], ident[0:NS + 1, 0:NS + 1])
        P.copy(SCT[:, k4 * 4:k4 * 4 + 4, :], pt[:, 0:128].re("p (j c) -> p j c", c=32)[:, :, 0:NS + 1], eng="act")

    def ada(wsrc, bsrc, dst, ncol):
        for n in range(ncol // 512):
            wf = WF[n % 2]
            P.dma("pool", wf[:, :, :], V(wsrc.ap[:, n * 512:(n + 1) * 512].rearrange("(k p) c -> p k c", p=128), []))
            bt = T5[n % 2]
            P.dma("sync", bt[0:NS + 1, 0:512], V(bsrc.ap[n * 512:(n + 1) * 512].partition_broadcast(NS + 1), []))
            pt = psum()
            for k in range(KC):
                P.mm(pt[0:NS + 1, 0:512], SCT[:, k, :], wf[:, k, :], start=(k == 0), stop=(k == KC - 1))
            ot = T5[2 + n % 2]
            P.tt(ot[0:NS + 1, 0:512], pt[0:NS + 1, 0:512], bt[0:NS + 1, 0:512], ALU.add)
            P.dma("sync", dst[:, n * 512:(n + 1) * 512], ot[0:NS + 1, 0:512])

    for l in range(L):
        ada(w_ada[l], b_ada[l], V(modd.t.ap()[l], [modd.dep]), 6 * D)
    ada(w_adaf[:, :], b_adaf[:], modf[:, :], 2 * D)

    P.dma("sync", LBT[:, 0, :, :], V(lbl.t.ap().rearrange("l (h p) -> p l h", p=128), []))
    P.act(LBT[:, 0, :, :], LBT[:, 0, :, :], AF.Exp)
    P.copy(LBT[:, 3, 0, :], LBT[:, 0, 0, :])
    for l in range(1, L):
        P.tt(LBT[:, 3, 0, :], LBT[:, 3, 0, :], LBT[:, 0, l, :], ALU.add)
    P.recip(LBT[:, 3, 0, :], LBT[:, 3, 0, :])
    P.memset(LBT[:, 1, 0, :], 0.0)
    for l in range(1, L):
        P.tt(LBT[:, 1, l, :], LBT[:, 0, l, :], LBT[:, 3, 0, :], ALU.mult)
        if l > 1:
            P.tt(LBT[:, 1, l, :], LBT[:, 1, l, :], LBT[:, 1, l - 1, :], ALU.add)
    for l in range(L):
        P.ts(LBT[:, 2, l, :], LBT[:, 1, l, :], -1.0, 1.0, ALU.mult, ALU.add)
    P.dma("sync", HNG[:, :], V(hng.t.ap().rearrange("l p -> p l"), []))
    P.dma("sync", PBS[:, 0, :, :], V(pb.t.ap().rearrange("l (j p) -> p l j", p=128), []))
    P.dma("sync", PBS[:, 1, :, :], V(psc.t.ap().rearrange("l (j p) -> p l j", p=128), []))
    P.barrier()

    def xkeys(kind, t0):
        return xres.r(*[(kind, t0, n) for n in range(4)]).deps

    def x_src(l, kind, t0, rows):
        if l == 0:
            return (xp[t0:t0 + rows, :] if kind == "p" else xs[0:rows, :])
        base = t0 if kind == "p" else T
        return V(xres.t.ap()[base:base + rows, :], xkeys(kind, t0))

    def x_dst(kind, t0, rows):
        base = t0 if kind == "p" else T
        return V(xres.t.ap()[base:base + rows, :], xkeys(kind, t0))

    def load_mod(dst, src2d, kind, j, rows, c0=0, c1=D):
        w = c1 - c0
        if kind == "p":
            P.dma("sync", dst[0:rows, 0:w], V(src2d.ap[0, j * D + c0:j * D + c1].partition_broadcast(rows), src2d.deps))
        else:
            for t in range(4):
                P.dma("sync", dst[t * NS:(t + 1) * NS, 0:w], V(src2d.ap[1:NS + 1, j * D + c0:j * D + c1], src2d.deps))

    def load_gain(dst, g1d, rows):
        P.dma("act", dst[0:rows, :], V(g1d.partition_broadcast(rows), []))

    cur_mod = [None]

    def prep_norm_mod(key, src2d, kind, jsh, jsc, g1d, rows):
        if cur_mod[0] == key:
            return
        cur_mod[0] = key
        load_mod(MA, src2d, kind, jsc, rows)
        load_gain(XT[1], g1d, rows)
        load_mod(MB, src2d, kind, jsh, rows)
        P.stt(MA[0:rows, :], MA[0:rows, :], 1.0, XT[1][0:rows, :], ALU.add, ALU.mult)

    def norm_mod(xt, rows, out):
        sq = XT[1]
        P.act(sq[0:rows, :], xt[0:rows, :], AF.Square)
        P.red(SM[0:rows, 0:1], sq[0:rows, :])
        P.act(SM[0:rows, 1:2], SM[0:rows, 0:1], AF.Sqrt, scale=1.0 / D, bias=EPS)
        P.recip(SM[0:rows, 2:3], SM[0:rows, 1:2])
        P.stt(out[0:rows, :], xt[0:rows, :], SM[0:rows, 2:3], MA[0:rows, :], ALU.mult, ALU.mult)
        P.tt(out[0:rows, :], out[0:rows, :], MB[0:rows, :], ALU.add)

    def to_hT(h, rows, c0):
        for k4 in range(4):
            pt = psum()
            for j in range(4):
                k = k4 * 4 + j
                P.tr(pt[0:128, j * 128:j * 128 + rows], h[0:rows, k * 128:(k + 1) * 128], ident[0:rows, 0:rows])
            src = pt[:, :].re("p (j c) -> p j c", c=128)[:, :, 0:rows]
            P.copy(HT[:, k4 * 4:k4 * 4 + 4, c0:c0 + rows], src, eng=("act" if k4 % 2 else "dve"))

    def load_w(slot, src2d, c0, ncol, dcol):
        P.dma("pool", WB[slot][:, :, dcol:dcol + ncol],
              V(src2d.ap[:, c0:c0 + ncol].rearrange("(k p) c -> p k c", p=128), []))

    def sg_cols(sg):
        cols = []
        c = 0
        for kind, t0, rows in sg:
            cols.append(c)
            c += rows
        return cols, c

    def slices_of(sg):
        out = []
        cols, _ = sg_cols(sg)
        i = 0
        while i < len(sg):
            kind = sg[i][0]
            if kind == "s":
                out.append(("s", cols[i], sg[i][2], sg[i][1]))
                i += 1
            else:
                j = i
                n = 0
                while j < len(sg) and sg[j][0] == "p" and n < 512:
                    n += sg[j][2]
                    j += 1
                out.append(("p", cols[i], n, sg[i][1]))
                i = j
        return out

    wslot = [0]

    for l in range(L):
        mod2d = V(modd.t.ap()[l], [modd.dep])
        P.dma("pool", PWB[:, :, :], V(pw.t.ap()[l].rearrange("g (ic p) o -> p (g ic) o", p=128), []))
        for h in range(NH):
            P.memset(S32[h][:, :], 0.0)
            P.memset(SBF[h][:, :], 0.0, eng="pool")
        P.memset(CAR[:, :, :], 0.0)
        for gi, sg in enumerate(sgs):
            cols, ntok = sg_cols(sg)
            slcs = slices_of(sg)
            last_sg = (gi == len(sgs) - 1)
            P.barrier()
            for ti, (kind, t0, rows) in enumerate(sg):
                prep_norm_mod((l, 1, kind), mod2d, kind, 0, 1, n1g.t.ap()[l], rows)
                xt = XT[0] if ti % 2 == 0 else XT2
                P.dma("sync", xt[0:rows, :], x_src(l, kind, t0, rows))
                norm_mod(xt, rows, xt)
                to_hT(xt, rows, cols[ti])
            P.barrier()
            for hp in range(4):
                slot = wslot[0] % 2
                wslot[0] += 1
                for ty in range(4):
                    load_w(slot, w_in[l], ty * 1024 + hp * 256, 256, ty * 256)
                W = WB[slot]
                for (kind, c0, ncol, tok0) in slcs:
                    C = 32 if kind == "p" else 4
                    nch = ncol // C if kind == "p" else NS

                    def chcols(ci, base=c0, kind=kind):
                        if kind == "p":
                            return slice(base + ci * 32, base + ci * 32 + 32)
                        return slice(base + ci, base + ci + 3 * NS + 1, NS)

                    def lcols(ci, kind=kind):
                        if kind == "p":
                            return slice(ci * 32, ci * 32 + 32)
                        return slice(ci, ci + 3 * NS + 1, NS)
                    for hv in range(2):
                        pv_ = psum()
                        for k in range(KC):
                            P.mm(pv_[:, 0:ncol], W[:, k, 512 + hv * 128:512 + hv * 128 + 128], HT[:, k, c0:c0 + ncol],
                                 start=(k == 0), stop=(k == KC - 1))
                        vT = T5[8]
                        P.act(vT[:, 0:ncol], pv_[:, 0:ncol], AF.Copy)
                        for ci in range(nch):
                            if ci % 4 == 0:
                                pt = psum()
                            P.tr(pt[0:C, (ci % 4) * 128:(ci % 4) * 128 + 128], vT[:, lcols(ci)], ident)
                            if ci % 4 == 3 or ci == nch - 1:
                                n4 = ci % 4 + 1
                                P.copy(VTK[0:C, ci - n4 + 1:ci + 1, hv * 128:hv * 128 + 128],
                                       pt[0:C, 0:n4 * 128].re("p (j c) -> p j c", c=128), eng=("act" if (ci // 4) % 2 else "dve"))
                    for hh in range(2):
                        h = hp * 2 + hh
                        zq = psum(); zf = psum(); zg = psum()
                        for (z, ty) in ((zq, 0), (zf, 1), (zg, 3)):
                            for k in range(KC):
                                P.mm(z[:, 0:ncol], W[:, k, ty * 256 + hh * 128:ty * 256 + hh * 128 + 128],
                                     HT[:, k, c0:c0 + ncol], start=(k == 0), stop=(k == KC - 1))
                        q = T5[0]; sgl = T5[1]; s1 = T5[2]; f = T5[2]; s2 = T5[3]; G = T5[4]; eG = T5[5]; emG = T5[6]
                        kt32 = T5[6]; kd = T5[4]
                        qt = B5[0]; kt = B5[1]
                        N = slice(0, ncol)
                        P.act(q[:, N], zq[:, N], AF.Silu)
                        P.act(sgl[:, N], zg[:, N], AF.Silu)
                        P.act(s1[:, N], zf[:, N], AF.Sigmoid)
                        P.act(s2[:, N], zf[:, N], AF.Sigmoid, scale=-1.0)
                        P.ts(f[:, N], s1[:, N], LBT[:, 2, l, h:h + 1], LBT[:, 1, l, h:h + 1], ALU.mult, ALU.add)
                        P.ts(s2[:, N], s2[:, N], LBT[:, 2, l, h:h + 1], None, ALU.mult)
                        P.act(f[:, N], f[:, N], AF.Ln)
                        if kind == "p":
                            P.op("dve", lambda e, G=G, f=f, N=N: e.tensor_tensor_scan(
                                out=G[:, N].ap, data0=mask64[:, N].ap, data1=f[:, N].ap, initial=0.0,
                                op0=ALU.mult, op1=ALU.add), [G[:, N]], [mask64[:, N], f[:, N]])
                        else:
                            P.copy(G[:, 0:NS], f[:, 0:NS])
                            for t in range(1, 4):
                                P.tt(G[:, t * NS:(t + 1) * NS], G[:, (t - 1) * NS:t * NS], f[:, t * NS:(t + 1) * NS], ALU.add)
                        P.act(eG[:, N], G[:, N], AF.Exp)
                        P.act(emG[:, N], G[:, N], AF.Exp, scale=-1.0)
                        P.tt(qt[:, N], q[:, N], eG[:, N], ALU.mult)
                        P.tt(kt32[:, N], s2[:, N], emG[:, N], ALU.mult)
                        P.copy(kt[:, N], kt32[:, N], eng="pool")
                        if kind == "p":
                            egc = eG[:, N].re("p (c t) -> p c t", t=32)[:, :, 31:32]
                            P.tt(kd[:, N].re("p (c t) -> p c t", t=32), kt32[:, N].re("p (c t) -> p c t", t=32),
                                 egc.bc([128, nch, 32]), ALU.mult)
                        else:
                            egc = eG[:, 3 * NS:4 * NS]
                            P.tt(kd[:, N].re("p (t s) -> p t s", s=NS), kt32[:, N].re("p (t s) -> p t s", s=NS),
                                 egc.unsq(1).bc([128, 4, NS]), ALU.mult)
                        if l == 0 and gi == 0 and h == 0 and kind == "p" and c0 == 0:
                            dump(f[:, 0:128], 128, 128); dump(G[:, 0:128], 128, 128) if False else None
                            dump(eG[:, 0:128], 128, 128); dump(kt32[:, 0:128], 128, 128); dump(kd[:, 0:128], 128, 128)
                        for ci in range(nch):
                            if ci % 4 == 0:
                                pt = psum()
                            P.tr(pt[0:C, (ci % 4) * 128:(ci % 4) * 128 + 128], kd[:, lcols(ci)], ident)
                            if ci % 4 == 3 or ci == nch - 1:
                                n4 = ci % 4 + 1
                                P.copy(KDT[0:C, ci - n4 + 1:ci + 1, :], pt[0:C, 0:n4 * 128].re("p (j c) -> p j c", c=128),
                                       eng="act")
                        pa = psum()
                        for ci in range(nch):
                            P.mm(pa[0:C, ci * C:(ci + 1) * C], kt[:, lcols(ci)], qt[:, lcols(ci)])
                        P.tt(ATM[0:C, 0:nch * C].re("p (c t) -> p c t", t=C),
                             pa[0:C, 0:nch * C].re("p (c t) -> p c t", t=C),
                             tri[0:C, 0:C].unsq(1).bc([C, nch, C]), ALU.mult)
                        atm = ATM[0:C, :]
                        po = PSB[6]; po2 = PSB[7]
                        hc = slice(hh * 128, hh * 128 + 128)
                        if kind == "p":
                            for ci in range(nch):
                                pd = psum()
                                P.mm(pd[:, 0:128], KDT[0:C, ci, :], VTK[0:C, ci, hc])
                                P.copy(DS.s(ci)[:, ci, :], pd[:, 0:128], eng="act")
                            for ci in range(nch):
                                prev = S32[h][:, :] if ci == 0 else SH.s(ci - 1)[:, ci - 1, :]
                                P.stt(SH.s(ci)[:, ci, :], prev, eG[:, ci * 32 + 31:ci * 32 + 32], DS.s(ci)[:, ci, :], ALU.mult, ALU.add)
                                P.copy(SHB.s(ci)[:, ci, :], SH.s(ci)[:, ci, :], eng="act")
                            for ci in range(nch):
                                sb_prev = SBF[h][:, :] if ci == 0 else SHB.s(ci - 1)[:, ci - 1, :]
                                P.mm(po[:, lcols(ci)], VTK[0:C, ci, hc], atm[:, ci * C:(ci + 1) * C])
                                P.mm(po2[:, lcols(ci)], sb_prev, qt[:, lcols(ci)])
                            P.copy(S32[h][:, :], SH.s(nch - 1)[:, nch - 1, :])
                            P.copy(SBF[h][:, :], SHB.s(nch - 1)[:, nch - 1, :], eng="pool")
                        else:
                            P.dma("sync", DS[:, 0:NS, :], V(sh_in.t.ap()[l, :, h].rearrange("s p v -> p s v"), []))
                            P.copy(SHB[:, 0:NS, :], DS[:, 0:NS, :], eng="pool")
                            for ci in range(nch):
                                P.mm(po[:, lcols(ci)], VTK[0:C, ci, hc], atm[:, ci * C:(ci + 1) * C])
                                P.mm(po2[:, lcols(ci)], SHB[:, ci, :], qt[:, lcols(ci)])
                                pd = psum()
                                P.mm(pd[:, 0:128], KDT[0:C, ci, :], VTK[0:C, ci, hc])
                                P.stt(SH.s(ci)[:, ci, :], DS[:, ci, :], eG[:, 3 * NS + ci:3 * NS + ci + 1], pd[:, 0:128], ALU.mult, ALU.add)
                            P.dma("sync", V(nhs.t.ap()[l, :, h].rearrange("s p v -> p s v"), []), SH[:, 0:NS, :])
                        if kind == "p" and last_sg and (c0 + ncol == sum(r for k_, _, r in sg if k_ == "p")):
                            P.dma("sync", nhp[l, h], S32[h][:, :])
                        if l == 0 and gi == 0 and h == 0 and kind == "p" and c0 == 0:
                            dump(S32[h][:, :], 128, 128)
                            dump(po[:, 0:128], 128, 128) if False else None
                        o32 = T5[0]; sq = T5[2]; rs = T5[3]
                        P.act(o32[:, N], po[:, N], AF.Copy)
                        P.tt(o32[:, N], o32[:, N], po2[:, N], ALU.add)
                        P.act(sq[:, N], o32[:, N], AF.Square)
                        pn = psum()
                        P.mm(pn[:, N], ones, sq[:, N])
                        P.act(rs[:, N], pn[:, N], AF.Sqrt, scale=1.0 / 128, bias=EPS)
                        P.recip(rs[:, N], rs[:, N])
                        P.tt(o32[:, N], o32[:, N], rs[:, N], ALU.mult)
                        oa = B5[2]
                        P.stt(oa[:, N], o32[:, N], HNG[:, l:l + 1], sgl[:, N], ALU.mult, ALU.mult)
                        P.dma("sync", V(mixTd.t.ap()[h * 128:(h + 1) * 128, c0:c0 + ncol], mixTd.r(h).deps), oa[:, N])
            slot = wslot[0] % 2
            wslot[0] += 1
            load_w(slot, w_in[l], 4096, 1024, 0)
            W = WB[slot]
            for (kind, c0, ncol, tok0) in slcs:
                if kind == "s":
                    for j in range(15):
                        r = RR[j // 8]
                        P.dma("sync", r[(j % 8) * NS:(j % 8 + 1) * NS, :], sp_in[l, :, j, :])
                pbt = {}
                for j in range(8):
                    g = j // 2
                    w = 2 << g
                    pz = psum()
                    for k in range(KC):
                        P.mm(pz[:, 0:ncol], W[:, k, j * 128:(j + 1) * 128], HT[:, k, c0:c0 + ncol],
                             start=(k == 0), stop=(k == KC - 1))
                    if kind == "p":
                        X = XSL; pre = 15; sh0 = 1
                        P.copy(X[:, 0:15], CAR[:, j, :])
                        P.copy(X[:, 15:15 + ncol], pz[:, 0:ncol], eng="act")
                        P.copy(CAR[:, j, :], X[:, ncol:ncol + 15])
                    else:
                        X = XSS; pre = 15 * NS; sh0 = NS
                        n0 = 8 * NS
                        n1 = 7 * NS
                        pt = psum()
                        P.tr(pt[:, 0:n0], RR[0][0:n0, j * 128:(j + 1) * 128], ident[0:n0, 0:n0])
                        P.tr(pt[:, n0:n0 + n1], RR[1][0:n1, j * 128:(j + 1) * 128], ident[0:n1, 0:n1])
                        P.copy(X[:, 0:pre], pt[:, 0:pre])
                        P.copy(X[:, pre:pre + ncol], pz[:, 0:ncol], eng="act")
                    tot = pre + ncol
                    s = X
                    bufs = [PS1, PS2]
                    for i in range(g + 1):
                        shf = sh0 << i
                        lo = sh0 * ((2 << i) - 1)
                        d = bufs[i % 2]
                        P.tt(d[:, lo:tot], s[:, lo:tot], s[:, lo - shf:tot - shf], ALU.add)
                        s = d
                    pl = T5[7]
                    P.stt(pl[:, 0:ncol], s[:, pre:tot], 1.0 / w, X[:, pre:tot], ALU.mult, ALU.subtract)
                    if kind == "p" and tok0 == 0:
                        P.tt(s[:, pre:pre + w - 1], s[:, pre:pre + w - 1], rcnt[:, 0:w - 1], ALU.mult)
                        P.tt(pl[:, 0:w - 1], s[:, pre:pre + w - 1], X[:, pre:pre + w - 1], ALU.subtract)
                    pbj = B5[3 + j % 2]
                    P.copy(pbj[:, 0:ncol], pl[:, 0:ncol], eng="pool")
                    pbt[j] = pbj
                    if kind == "s":
                        pt = psum()
                        n0 = 8 * NS
                        n1 = 7 * NS
                        P.tr(pt[0:n0, 0:128], X[:, 4 * NS:4 * NS + n0], ident)
                        P.tr(pt[0:n1, 128:256], X[:, 4 * NS + n0:4 * NS + n0 + n1], ident)
                        P.copy(RR[0][0:n0, j * 128:(j + 1) * 128], pt[0:n0, 0:128], eng="act")
                        P.copy(RR[1][0:n1, j * 128:(j + 1) * 128], pt[0:n1, 128:256], eng="act")
                    elif last_sg and (c0 + ncol == sum(r for k_, _, r in sg if k_ == "p")):
                        pt = psum()
                        P.tr(pt[0:15, 0:128], CAR[:, j, :], ident)
                        P.copy(RR[0][0:15, j * 128:(j + 1) * 128], pt[0:15, 0:128], eng="act")
                    if j % 2 == 1:
                        for oc in range(2):
                            pq = psum()
                            for ic in range(2):
                                P.mm(pq[:, 0:ncol], PWB[:, g * 2 + ic, oc * 128:(oc + 1) * 128], pbt[2 * g + ic][:, 0:ncol],
                                     start=(ic == 0), stop=(ic == 1))
                            ob = B5[5] if oc else B5[2]
                            jj = g * 2 + oc
                            P.ts(ob[:, 0:ncol], pq[:, 0:ncol], PBS[:, 0, l, jj:jj + 1], PBS[:, 1, l, jj:jj + 1], ALU.add, ALU.mult)
                            P.dma("sync", V(mixTd.t.ap()[1024 + jj * 128:1024 + (jj + 1) * 128, c0:c0 + ncol],
                                            mixTd.r(8 + jj).deps), ob[:, 0:ncol])
                if kind == "s":
                    for j in range(15):
                        r = RR[j // 8]
                        P.dma("sync", nps[l, :, j, :], r[(j % 8) * NS:(j % 8 + 1) * NS, :])
                elif last_sg and (c0 + ncol == sum(r for k_, _, r in sg if k_ == "p")):
                    P.dma("sync", npp[l], RR[0][0:15, :])
            P.barrier()
            for k in range(KC):
                P.dma("sync", HT[:, k, 0:ntok], V(mixTd.t.ap()[k * 128:(k + 1) * 128, 0:ntok], mixTd.r(k).deps))
            ptiles = [(ti, t0) for ti, (kind, t0, rows) in enumerate(sg) if kind == "p"]
            stiles = [(ti, t0, rows) for ti, (kind, t0, rows) in enumerate(sg) if kind == "s"]
            npt = len(ptiles)
            pt0 = ptiles[0][1]
            xsrc_p = xp.t.ap() if l == 0 else xres.t.ap()
            for n in range(4):
                slot = wslot[0] % 2
                wslot[0] += 1
                load_w(slot, w_out[l], n * 512, 512, 0)
                W = WB[slot]
                cs = slice(n * 512, (n + 1) * 512)
                pdeps = [d for (_, t0) in ptiles for d in xres.r(("p", t0, n)).deps]
                g1p = T5[8]
                load_mod(g1p, mod2d, "p", 2, 128, n * 512, (n + 1) * 512)
                P.dma("sync", XB[:, 0:npt, :], V(xsrc_p[pt0:pt0 + 128 * npt, cs].rearrange("(j p) c -> p j c", p=128),
                                                 pdeps if l > 0 else []))
                for j, (ti, t0) in enumerate(ptiles):
                    pm = psum()
                    for k in range(KC):
                        P.mm(pm[:, :], HT[:, k, cols[ti]:cols[ti] + 128], W[:, k, 0:512], start=(k == 0), stop=(k == KC - 1))
                    xo = GE[j % 2]
                    P.tt(xo[:, :], pm[:, :], g1p[:, :], ALU.mult)
                    P.tt(XB[:, j, :], XB[:, j, :], xo[:, :], ALU.add)
                P.dma("sync", V(xres.t.ap()[pt0:pt0 + 128 * npt, cs].rearrange("(j p) c -> p j c", p=128), pdeps), XB[:, 0:npt, :])
                for (ti, t0, rows) in stiles:
                    g1s = B5F[0]; xin = B5F[1]
                    load_mod(g1s, mod2d, "s", 2, rows, n * 512, (n + 1) * 512)
                    xs_v = x_src(l, "s", t0, rows)
                    P.dma("sync", xin[0:rows, :], V(xs_v.ap[:, cs], xs_v.deps))
                    pm = psum()
                    for k in range(KC):
                        P.mm(pm[0:rows, :], HT[:, k, cols[ti]:cols[ti] + rows], W[:, k, 0:512], start=(k == 0), stop=(k == KC - 1))
                    P.tt(g1s[0:rows, :], pm[0:rows, :], g1s[0:rows, :], ALU.mult)
                    P.tt(xin[0:rows, :], xin[0:rows, :], g1s[0:rows, :], ALU.add)
                    P.dma("sync", V(xres.t.ap()[T:T + rows, cs], xres.r(("s", t0, n)).deps), xin[0:rows, :])
            P.barrier()
            for ti, (kind, t0, rows) in enumerate(sg):
                prep_norm_mod((l, 2, kind), mod2d, kind, 3, 4, n2g.t.ap()[l], rows)
                xt = XT[0] if ti % 2 == 0 else XT2
                base = t0 if kind == "p" else T
                P.dma("sync", xt[0:rows, :], x_dst(kind, t0, rows))
                norm_mod(xt, rows, xt)
                to_hT(xt, rows, cols[ti])
            P.barrier()
            wslot[0] += (wslot[0] % 2)
            P.dma("sync", KT[:, :, :], keysT[l])
            load_w(0, wq[l], 0, 1024, 0)
            load_w(1, wq[l], 1024, 1024, 0)
            wslot[0] += 2
            for ti, (kind, t0, rows) in enumerate(sg):
                R = slice(0, rows)
                for n in range(4):
                    W = WB[n // 2]
                    pqr = psum()
                    for k in range(KC):
                        P.mm(pqr[R, :], HT[:, k, cols[ti]:cols[ti] + rows], W[:, k, (n % 2) * 512:(n % 2) * 512 + 512],
                             start=(k == 0), stop=(k == KC - 1))
                    qs = T5[0]; sq = T5[1]
                    P.act(qs[R, :], pqr[R, :], AF.Copy)
                    P.act(sq[R, :], pqr[R, :], AF.Square)
                    P.red(SM[R, 8:12], sq[R, :].re("p (j c) -> p j c", c=128))
                    P.act(SM[R, 12:16], SM[R, 8:12], AF.Sqrt, scale=1.0 / 128, bias=EPS)
                    P.recip(SM[R, 16:20], SM[R, 12:16])
                    pt = psum()
                    for j in range(4):
                        P.tr(pt[:, j * 128:j * 128 + rows], qs[R, j * 128:(j + 1) * 128], ident[R, R])
                    qT = T5[2]
                    P.copy(qT[:, :], pt[:, :], eng="act")
                    psc_ = psum()
                    for j in range(4):
                        P.mm(psc_[R, j * 128:(j + 1) * 128], qT[:, j * 128:j * 128 + rows], KT[:, n * 4 + j, :])
                    P.tt(SC[R, n * 4:n * 4 + 4, :], psc_[R, :].re("p (j c) -> p j c", c=128),
                         SM[R, 16:20].unsq(2).bc([rows, 4, 128]), ALU.mult)
                for g in range(16):
                    P.op("dve", lambda e, g=g, R=R: e.max(out=V1.s(g)[R, g, 0:8].ap, in_=SC[R, g, :].ap), [V1.s(g)[R, g, 0:8]], [SC[R, g, :]])
                for g in range(16):
                    P.op("dve", lambda e, g=g, R=R: e.match_replace(out=SCW.s(g)[R, g, :].ap, in_to_replace=V1.s(g)[R, g, 0:8].ap,
                                                               in_values=SC[R, g, :].ap, imm_value=NEG),
                         [SCW.s(g)[R, g, :]], [V1.s(g)[R, g, 0:8], SC[R, g, :]])
                for g in range(16):
                    P.op("dve", lambda e, g=g, R=R: e.max(out=V1.s(g)[R, g, 8:16].ap, in_=SCW.s(g)[R, g, :].ap), [V1.s(g)[R, g, 8:16]], [SCW.s(g)[R, g, :]])
                for g in range(16):
                    P.op("dve", lambda e, g=g, R=R: e.max_index(out=I1U.s(g)[R, g, 0:8].ap, in_max=V1.s(g)[R, g, 0:8].ap, in_values=SC[R, g, :].ap),
                         [I1U.s(g)[R, g, 0:8]], [V1.s(g)[R, g, 0:8], SC[R, g, :]])
                for g in range(16):
                    P.op("dve", lambda e, g=g, R=R: e.max_index(out=I1U.s(g)[R, g, 8:16].ap, in_max=V1.s(g)[R, g, 8:16].ap, in_values=SCW.s(g)[R, g, :].ap),
                         [I1U.s(g)[R, g, 8:16]], [V1.s(g)[R, g, 8:16], SCW.s(g)[R, g, :]])
                P.copy(I1F[R, :, :], I1U[R, :, :])
                if l == 0 and gi == 0 and ti == 0:
                    dump(SC[:, :, :].re("p a b -> p (a b)"), 128, 2048, name="d_SC")
                    dump(V1[:, :, :].re("p a b -> p (a b)"), 128, 256, name="d_V1a")
                v1v = V1[R, :, :].re("p (h q) k -> p h q k", q=2)
                P.tt(CA[R, :, :].re("p h (a b) -> p h a b", b=16), v1v[:, :, 0, :].unsq(3).bc([rows, 8, 16, 16]),
                     v1v[:, :, 1, :].unsq(2).bc([rows, 8, 16, 16]), ALU.add)
                for h in range(8):
                    P.op("dve", lambda e, h=h, R=R: e.max(out=TS_.s(h)[R, h, 0:8].ap, in_=CA[R, h, :].ap), [TS_.s(h)[R, h, 0:8]], [CA[R, h, :]])
                for h in range(8):
                    P.op("dve", lambda e, h=h, R=R: e.match_replace(out=CAW.s(h)[R, h, :].ap, in_to_replace=TS_.s(h)[R, h, 0:8].ap,
                                                               in_values=CA[R, h, :].ap, imm_value=NEG),
                         [CAW.s(h)[R, h, :]], [TS_.s(h)[R, h, 0:8], CA[R, h, :]])
                for h in range(8):
                    P.op("dve", lambda e, h=h, R=R: e.max(out=TS_.s(h)[R, h, 8:16].ap, in_=CAW.s(h)[R, h, :].ap), [TS_.s(h)[R, h, 8:16]], [CAW.s(h)[R, h, :]])
                for h in range(8):
                    P.op("dve", lambda e, h=h, R=R: e.max_index(out=PO.s(h)[R, h, 0:8].ap, in_max=TS_.s(h)[R, h, 0:8].ap, in_values=CA[R, h, :].ap),
                         [PO.s(h)[R, h, 0:8]], [TS_.s(h)[R, h, 0:8], CA[R, h, :]])
                for h in range(8):
                    P.op("dve", lambda e, h=h, R=R: e.max_index(out=PO.s(h)[R, h, 8:16].ap, in_max=TS_.s(h)[R, h, 8:16].ap, in_values=CAW.s(h)[R, h, :].ap),
                         [PO.s(h)[R, h, 8:16]], [TS_.s(h)[R, h, 8:16], CAW.s(h)[R, h, :]])
                P.tt(GT[R, :, :], TS_[R, :, :], TS_[R, :, 0:1].bc([rows, 8, 16]), ALU.subtract)
                P.act(GT[R, :, :], GT[R, :, :], AF.Exp)
                P.red(SM[R, 24:32], GT[R, :, :])
                P.recip(SM[R, 32:40], SM[R, 24:32])
                P.tt(GT[R, :, :], GT[R, :, :], SM[R, 32:40].unsq(2).bc([rows, 8, 16]), ALU.mult)
                P.copy(K2F[R, :, :], PO[R, :, :])
                P.ts(SM[R, 40:56], iota16[R, :], 16.0, 16.0, ALU.mult, ALU.add)
                P.tt(OH[R, :, :, :], K2F[R, :, :].unsq(3).bc([rows, 8, 16, 16]),
                     SM[R, 40:56].unsq(1).unsq(1).bc([rows, 8, 16, 16]), ALU.is_ge)
                P.red(K1F[R, :, :], OH[R, :, :, :])
                P.stt(K2F[R, :, :], K1F[R, :, :], -16.0, K2F[R, :, :], ALU.mult, ALU.add)
                i1v = I1F[R, :, :].re("p (h q) k -> p h q k", q=2)
                for (KF, qq, IO) in ((K1F, 0, IO1), (K2F, 1, IO2)):
                    P.tt(OH[R, :, :, :], KF[R, :, :].unsq(3).bc([rows, 8, 16, 16]),
                         iota16[R, :].unsq(1).unsq(1).bc([rows, 8, 16, 16]), ALU.is_equal)
                    P.tt(OH[R, :, :, :], OH[R, :, :, :], i1v[:, :, qq, :].unsq(2).bc([rows, 8, 16, 16]), ALU.mult)
                    P.red(IO[R, :, :], OH[R, :, :, :])
                if l == 0 and gi == 0 and ti == 0:
                    dump(I1F[:, :, :].re("p a b -> p (a b)"), 128, 256, name="d_I1F")
                    dump(V1[:, :, :].re("p a b -> p (a b)"), 128, 256, name="d_V1")
                    dump(TS_[:, :, :].re("p a b -> p (a b)"), 128, 128, name="d_TS")
                    dump(PO[:, :, :].re("p a b -> p (a b)"), 128, 128, dt=U32, name="d_PO")
                    dump(K1F[:, :, :].re("p a b -> p (a b)"), 128, 128, name="d_K1F")
                    dump(K2F[:, :, :].re("p a b -> p (a b)"), 128, 128, name="d_K2F")
                    dump(IO1[:, :, :].re("p a b -> p (a b)"), 128, 128, name="d_IO1")
                    dump(IO2[:, :, :].re("p a b -> p (a b)"), 128, 128, name="d_IO2")
                    dump(GT[:, :, :].re("p a b -> p (a b)"), 128, 128, name="d_GT")
                pt = psum()
                for j, src in enumerate((IO1, IO2, GT)):
                    P.tr(pt[:, j * 128:j * 128 + rows], src[R, :, :].re("p h k -> p (h k)"), ident[R, R])
                for j in range(3):
                    P.copy(RT[j][:, cols[ti]:cols[ti] + rows], pt[:, j * 128:j * 128 + rows], eng="act")
            P.barrier()
            P.ts(RT[1][:, 0:ntok], RT[1][:, 0:ntok], -1.0, None, ALU.mult)
            for ti, (kind, t0, rows) in enumerate(sg):
                for tk0 in range(0, rows, 16):
                    qd = PSQV[(tk0 // 16) % 2]
                    for j in range(16):
                        tk = tk0 + j
                        col = cols[ti] + tk
                        oa = OA[tk % 6]; ob = OB[tk % 6]
                        P.ts(oa[:, :], iota, RT[0][:, col:col + 1], RT[2][:, col:col + 1], ALU.is_equal, ALU.mult)
                        if tk % 3 == 0:
                            ab = OAB[(tk // 3) % 3]
                            P.op("act", lambda e, ab=ab, col=col: e.activation(out=ab[:, :].ap, in_=iota.ap, func=AF.Abs, scale=1.0,
                                                                            bias=RT[1][:, col:col + 1].ap), [ab[:, :]], [iota, RT[1][:, col:col + 1]])
                            P.act(ob[:, :], ab[:, :], AF.Relu, scale=-1.0, bias=1.0)
                        else:
                            P.ts(ob[:, :], iota, -1.0, RT[1][:, col:col + 1], ALU.mult, ALU.is_equal)
                        bq, t4 = j // 4, j % 4
                        P.mm(qd[:, bq * 512 + t4:bq * 512 + 512:4], ob[:, :], oa[:, :])
                    P.copy(WST[:, :, tk0:tk0 + 16].re("p i (b t) -> p i b t", t=4),
                           qd[:, :].re("p (b i t) -> p i b t", b=4, t=4), eng="act")
                if l == 0 and gi == 0 and ti == 0:
                    for j in range(3):
                        dump(RT[j][:, 0:128], 128, 128, name=f"d_RT{j}")
                    dump(WST[:, :, 70], 128, 128, dt=BF16, name="d_W70")
                    dump(WST[:, :, 5], 128, 128, dt=BF16, name="d_W5")
                for c8 in range(8):
                    P.dma("sync", V(Wd.t.ap()[c8 * 16:(c8 + 1) * 16, :, cols[ti]:cols[ti] + rows].rearrange("c i t -> i c t"),
                                    Wd.r(c8).deps), WST[:, c8 * 16:(c8 + 1) * 16, 0:rows])
            P.barrier()
            tgs = []
            c = 0
            while c < ntok:
                n = min(512, ntok - c)
                tgs.append((c, n))
                c += n
            CG = 4
            for cg in range(NK // CG):
                P.dma("pool", UC[:, :, :, :], V(uTb.t.ap()[l, cg * CG:(cg + 1) * CG].rearrange("c p k e -> p c k e"), []))
                P.dma("pool", VC[:, :, :], V(pv.t.ap()[l, cg * CG * 128:(cg + 1) * CG * 128, :].rearrange("(c p) d -> p c d", p=128), []))
                P.dma("sync", WC[:, :, 0:ntok], V(Wd.t.ap()[cg * CG:(cg + 1) * CG, :, 0:ntok].rearrange("c i t -> i c t"),
                                                  Wd.r(cg * CG // 16).deps))
                for ci in range(CG):
                    for (tc0, tn) in tgs:
                        pu = psum()
                        for k in range(KC):
                            P.mm(pu[:, 0:tn], UC[:, ci, k, :], HT[:, k, tc0:tc0 + tn], start=(k == 0), stop=(k == KC - 1))
                        ge = GE[(ci + tc0 // 512) % 2]
                        P.act(ge[:, 0:tn], pu[:, 0:tn], AF.Gelu)
                        P.tt(AT[:, ci, tc0:tc0 + tn], ge[:, 0:tn], WC[:, ci, tc0:tc0 + tn], ALU.mult)
                for ti, (kind, t0, rows) in enumerate(sg):
                    for dq in range(4):
                        pp = psum()
                        for ci in range(CG):
                            P.mm(pp[0:rows, :], AT[:, ci, cols[ti]:cols[ti] + rows], VC[:, ci, dq * 512:(dq + 1) * 512],
                                 start=(ci == 0), stop=(ci == CG - 1))
                        a = ACC[0:rows, ti, dq * 512:(dq + 1) * 512]
                        if cg == 0:
                            P.copy(a, pp[0:rows, :])
                        else:
                            P.tt(a, a, pp[0:rows, :], ALU.add)
            P.barrier()
            gk = None
            for ti, (kind, t0, rows) in enumerate(sg):
                if gk != kind:
                    gk = kind
                    load_mod(MA, mod2d, kind, 5, rows)
                    cur_mod[0] = None
                xt = XT[0] if ti % 2 == 0 else XT2
                base = t0 if kind == "p" else T
                P.dma("sync", xt[0:rows, :], x_dst(kind, t0, rows))
                P.tt(ACC[0:rows, ti, :], ACC[0:rows, ti, :], MA[0:rows, :], ALU.mult)
                if dbg != 2:
                    P.tt(xt[0:rows, :], xt[0:rows, :], ACC[0:rows, ti, :], ALU.add)
                if l < L - 1:
                    P.dma("sync", x_dst(kind, t0, rows), xt[0:rows, :])
                else:
                    P.dma("sync", x_dst(kind, t0, rows), xt[0:rows, :])
            if l == L - 1:
                for ti, (kind, t0, rows) in enumerate(sg):
                    prep_norm_mod(("f", kind), modf[:, :], kind, 0, 1, fg.t.ap(), rows)
                    xt = XT[0] if ti % 2 == 0 else XT2
                    P.dma("sync", xt[0:rows, :], x_dst(kind, t0, rows))
                    norm_mod(xt, rows, xt)
                    dst = yp[t0:t0 + rows, :] if kind == "p" else ys[0:rows, :]
                    P.dma("sync", dst, xt[0:rows, :])
    P.emit()
    return nc


def host_consts():
    c = np.zeros((128, 1024), np.float32)
    c[:, 0:128] = np.eye(128, dtype=np.float32)
    c[:, 128:256] = np.arange(128, dtype=np.float32)[None, :]
    c[0:64, 256:320] = np.triu(np.ones((64, 64), np.float32))
    c[:, 320:448] = 1.0
    c[:, 448:464] = 1.0 / np.arange(1, 17, dtype=np.float32)[None, :]
    m = np.ones(512, np.float32)
    m[::32] = 0.0
    c[:, 512:1024] = m[None, :]
    return c


def make_in_maps(inp, T, NS, L, n_cores, n_prompt):
    f = lambda a: np.ascontiguousarray(np.asarray(a, dtype=np.float32))
    keysT = f(np.asarray(inp["peer_keys"]).reshape(L, 16, NK, 128).transpose(0, 3, 1, 2))
    u = np.asarray(inp["peer_u"])
    uTb = f(u.reshape(L, NK, 128, KC, 128).transpose(0, 1, 4, 3, 2))
    shared = {
        "w_ada": f(inp["w_ada"]), "b_ada": f(inp["b_ada"]), "n1g": f(inp["norm1_g"]), "n2g": f(inp["norm2_g"]),
        "w_in": f(inp["w_in"]), "w_out": f(inp["w_out"]), "lbl": f(inp["lb_logits"]), "hng": f(inp["hgrn_norm_g"]),
        "pw": f(inp["pool_w"]), "pb": f(inp["pool_b"]), "psc": f(inp["pool_scale"]), "wq": f(inp["peer_wq"]),
        "keysT": keysT, "uTb": uTb, "pv": f(inp["peer_v"]), "fg": f(inp["final_g"]),
        "w_adaf": f(inp["w_ada_final"]), "b_adaf": f(inp["b_ada_final"]), "cst": host_consts(),
    }
    maps = []
    xp_all = np.asarray(inp["x_prompt"]); xs_all = np.asarray(inp["x_sample"])
    cp = np.asarray(inp["c_prompt"]); cs = np.asarray(inp["c_sample"])
    sh = np.asarray(inp["state_hgrn"]); sp = np.asarray(inp["state_pool"])
    for c in range(n_cores):
        b = c % n_prompt
        s0 = c * NS
        m = dict(shared)
        m["xp"] = f(xp_all[b])
        m["xs"] = f(xs_all[s0:s0 + NS].transpose(1, 0, 2).reshape(4 * NS, D))
        m["c17"] = f(np.concatenate([cp[b:b + 1], cs[s0:s0 + NS]], axis=0))
        m["sh_in"] = f(sh[:, s0:s0 + NS])
        m["sp_in"] = f(sp[:, s0:s0 + NS])
        maps.append(m)
    return maps


def gather(res, T, NS, L, n_cores, n_prompt):
    y_p = np.stack([res[b]["yp"] for b in range(n_prompt)])
    y_s = np.concatenate([res[c]["ys"].reshape(4, NS, D).transpose(1, 0, 2) for c in range(n_cores)], axis=0)
    nh_p = np.stack([res[b]["nhp"] for b in range(n_prompt)], axis=1)
    np_p = np.stack([res[b]["npp"] for b in range(n_prompt)], axis=1)
    nh_s = np.concatenate([res[c]["nhs"] for c in range(n_cores)], axis=1)
    np_s = np.concatenate([res[c]["nps"] for c in range(n_cores)], axis=1)
    return tuple(np.ascontiguousarray(a, dtype=np.float32) for a in (y_p, y_s, nh_p, np_p, nh_s, np_s))


def kernel(**inputs):
    T = 2048; NS = 16; L = 2; n_cores = 8; n_prompt = 4
    nc = build(T, NS, L)
    in_maps = make_in_maps(inputs, T, NS, L, n_cores, n_prompt)
    res = run_bass_kernel_spmd(nc, in_maps, core_ids=list(range(n_cores)))
    return gather(res.results, T, NS, L, n_cores, n_prompt)
```
